# Optimizing a Trainium2 kernel written in Bass

```python
import math
import jax, jax.numpy as jnp
from jax import lax
import numpy as np

D_MODEL = 1024
BATCH = 2
SEQ = 8192
DEPTH = 4

CHUNK = 64
N_A_LAYERS = DEPTH // 2
N_B_LAYERS = DEPTH - N_A_LAYERS
A_INNER = 2 * D_MODEL
A_HEADS = 4
A_HEAD_DIM = A_INNER // A_HEADS
A_CONV = 4
B_INNER = D_MODEL
B_HEADS = 16
B_HEAD_DIM = B_INNER // B_HEADS
B_PAST_CHUNKS = 8
B_PAST = B_PAST_CHUNKS * CHUNK
B_BAND = (B_PAST_CHUNKS + 1) * CHUNK
REL_CLIP = 128
DEEPNORM_ALPHA = (2.0 * DEPTH) ** 0.25
DEEPNORM_BETA = (8.0 * DEPTH) ** -0.25
LN_EPS = 1e-5
GN_EPS = 1e-6

kernel_name = "mlstm_yoco_chunked_relpos_attention_deepnorm"


def layer_norm(x, g, b):
    xf = x.astype(jnp.float32)
    mu = jnp.mean(xf, axis=-1, keepdims=True)
    var = jnp.mean(jnp.square(xf - mu), axis=-1, keepdims=True)
    return ((xf - mu) * lax.rsqrt(var + LN_EPS) * g.astype(jnp.float32) + b.astype(jnp.float32)).astype(x.dtype)


def causal_depthwise_conv(x, w, b):
    k_w = w.shape[0]
    s = x.shape[1]
    xp = jnp.pad(x, ((0, 0), (k_w - 1, 0), (0, 0)))
    out = xp[:, 0:s] * w[0]
    for tap in range(1, k_w):
        out = out + xp[:, tap:tap + s] * w[tap]
    return out + b


def mlstm_chunkwise(q, k, v, i_pre, log_f):
    bsz, s, h, dh = q.shape
    nc = s // CHUNK

    def to_chunks(t):
        t = t.reshape((bsz, nc, CHUNK, h) + t.shape[3:])
        return jnp.moveaxis(t, (1, 3), (0, 2))

    xs = (to_chunks(q), to_chunks(k), to_chunks(v), to_chunks(i_pre), to_chunks(log_f))
    causal = jnp.tril(jnp.ones((CHUNK, CHUNK), dtype=bool))

    def step(carry, inp):
        c_mat, n_vec, m = carry
        q_, k_, v_, i_, lf = inp
        bcum = jnp.cumsum(lf, axis=-1)
        dmat = bcum[..., :, None] - bcum[..., None, :] + i_[..., None, :]
        dmat = jnp.where(causal, dmat, -jnp.inf)
        m_inter = bcum + m[..., None]
        m_t = jnp.maximum(m_inter, jnp.max(dmat, axis=-1))
        w_intra = jnp.exp(dmat - m_t[..., None])
        w_inter = jnp.exp(m_inter - m_t)
        sw = jnp.einsum('bhtd,bhsd->bhts', q_, k_) * w_intra
        num = jnp.einsum('bhts,bhsd->bhtd', sw, v_) + w_inter[..., None] * jnp.einsum('bhtk,bhkd->bhtd', q_, c_mat)
        den = jnp.sum(sw, axis=-1) + w_inter * jnp.einsum('bhtk,bhk->bht', q_, n_vec)
        den = jnp.maximum(jnp.abs(den), jnp.exp(-m_t))
        h_out = num / den[..., None]
        b_last = bcum[..., -1]
        g = b_last[..., None] - bcum + i_
        m_new = jnp.maximum(b_last + m, jnp.max(g, axis=-1))
        decay = jnp.exp(b_last + m - m_new)
        wg = jnp.exp(g - m_new[..., None])
        c_new = decay[..., None, None] * c_mat + jnp.einsum('bhs,bhsk,bhsd->bhkd', wg, k_, v_)
        n_new = decay[..., None] * n_vec + jnp.einsum('bhs,bhsk->bhk', wg, k_)
        return (c_new, n_new, m_new), h_out

    init = (jnp.zeros((bsz, h, dh, dh), jnp.float32),
            jnp.zeros((bsz, h, dh), jnp.float32),
            jnp.zeros((bsz, h), jnp.float32))
    _, hs = lax.scan(step, init, xs)
    return jnp.moveaxis(hs, (0, 2), (1, 3)).reshape(bsz, s, h, dh)


def mlstm_layer(x, w_in, b_gate, conv_w, conv_b, w_q, w_k, w_v, gn_w, skip, w_out):
    bsz, s, _ = x.shape
    u = x @ w_in
    xm, z, o_pre, gates = jnp.split(u, [A_INNER, 2 * A_INNER, 3 * A_INNER], axis=-1)
    gates = (gates + b_gate).astype(jnp.float32)
    i_pre = gates[..., :A_HEADS]
    log_f = jax.nn.log_sigmoid(gates[..., A_HEADS:])
    xc = jax.nn.silu(causal_depthwise_conv(xm, conv_w, conv_b))
    heads = lambda t: t.reshape(bsz, s, A_HEADS, A_HEAD_DIM)
    q = jnp.einsum('bshd,hde->bshe', heads(xc), w_q)
    k = jnp.einsum('bshd,hde->bshe', heads(xc), w_k) * (A_HEAD_DIM ** -0.5)
    v = jnp.einsum('bshd,hde->bshe', heads(xm), w_v)
    h = mlstm_chunkwise(q.astype(jnp.float32), k.astype(jnp.float32), v.astype(jnp.float32), i_pre, log_f)
    mu = jnp.mean(h, axis=-1, keepdims=True)
    var = jnp.mean(jnp.square(h - mu), axis=-1, keepdims=True)
    hn = ((h - mu) * lax.rsqrt(var + GN_EPS)).reshape(bsz, s, A_INNER).astype(x.dtype) * gn_w
    h = jax.nn.sigmoid(o_pre) * hn + skip * xc
    return (h * jax.nn.silu(z)) @ w_out


def shared_band_kv(x, kv_w):
    bsz, s, _ = x.shape
    k, v = jnp.split(x @ kv_w, 2, axis=-1)

    def prep(t):
        t = t.reshape(bsz, s, B_HEADS, B_HEAD_DIM).transpose(0, 2, 1, 3)
        return jnp.pad(t, ((0, 0), (0, 0), (B_PAST, 0), (0, 0)))

    return prep(k), prep(v)


def chunk_attn_layer(x, k_pad, v_pad, w_in, rel_bias, w_out):
    bsz, s, _ = x.shape
    nc = s // CHUNK
    q, g = jnp.split(x @ w_in, 2, axis=-1)
    q = q.reshape(bsz, s, B_HEADS, B_HEAD_DIM).transpose(0, 2, 1, 3) * (B_HEAD_DIM ** -0.5)
    rel = jnp.arange(CHUNK)[:, None] + B_PAST - jnp.arange(B_BAND)[None, :]
    bias = rel_bias[:, jnp.clip(rel, -REL_CLIP, REL_CLIP) + REL_CLIP].astype(jnp.float32)
    key_off = jnp.arange(B_BAND) - B_PAST

    def one_chunk(c):
        start = c * CHUNK
        q_c = lax.dynamic_slice_in_dim(q, start, CHUNK, axis=2)
        k_c = lax.dynamic_slice_in_dim(k_pad, start, B_BAND, axis=2)
        v_c = lax.dynamic_slice_in_dim(v_pad, start, B_BAND, axis=2)
        sc = jnp.einsum('bhqd,bhkd->bhqk', q_c, k_c).astype(jnp.float32) + bias
        valid = (start + key_off) >= 0
        sc = jnp.where(valid, sc, -jnp.inf)
        p = jax.nn.softmax(sc, axis=-1)
        return jnp.einsum('bhqk,bhkd->bhqd', p.astype(v_c.dtype), v_c)

    o = lax.map(one_chunk, jnp.arange(nc))
    o = o.transpose(1, 0, 3, 2, 4).reshape(bsz, s, B_INNER)
    return (o * jax.nn.silu(g)) @ w_out


def setup_inputs(seed: int = 0) -> dict:
    key = jax.random.key(seed)
    ks = jax.random.split(key, 24)
    f32 = jnp.float32
    nrm = lambda k, shape, scale: jax.random.normal(k, shape, f32) * scale
    x = jax.random.normal(ks[0], (BATCH, SEQ, D_MODEL), f32)
    a_w_in = nrm(ks[1], (N_A_LAYERS, D_MODEL, 3 * A_INNER + 2 * A_HEADS), D_MODEL ** -0.5)
    i_bias = nrm(ks[2], (N_A_LAYERS, A_HEADS), 0.1)
    f_bias = jnp.linspace(3.0, 6.0, A_HEADS, dtype=f32)[None, :] + nrm(ks[3], (N_A_LAYERS, A_HEADS), 0.1)
    a_b_gate = jnp.concatenate([i_bias, f_bias], axis=-1)
    a_conv_w = nrm(ks[4], (N_A_LAYERS, A_CONV, A_INNER), A_CONV ** -0.5)
    a_conv_b = nrm(ks[5], (N_A_LAYERS, A_INNER), 0.02)
    a_w_q = nrm(ks[6], (N_A_LAYERS, A_HEADS, A_HEAD_DIM, A_HEAD_DIM), A_HEAD_DIM ** -0.5)
    a_w_k = nrm(ks[7], (N_A_LAYERS, A_HEADS, A_HEAD_DIM, A_HEAD_DIM), A_HEAD_DIM ** -0.5)
    a_w_v = nrm(ks[8], (N_A_LAYERS, A_HEADS, A_HEAD_DIM, A_HEAD_DIM), A_HEAD_DIM ** -0.5)
    a_gn_w = 1.0 + nrm(ks[9], (N_A_LAYERS, A_INNER), 0.02)
    a_skip = 1.0 + nrm(ks[10], (N_A_LAYERS, A_INNER), 0.02)
    a_w_out = nrm(ks[11], (N_A_LAYERS, A_INNER, D_MODEL), DEEPNORM_BETA * A_INNER ** -0.5)
    a_ln_g = 1.0 + nrm(ks[12], (N_A_LAYERS, D_MODEL), 0.02)
    a_ln_b = nrm(ks[13], (N_A_LAYERS, D_MODEL), 0.02)
    kv_w = nrm(ks[14], (D_MODEL, 2 * B_INNER), D_MODEL ** -0.5)
    b_w_in = nrm(ks[15], (N_B_LAYERS, D_MODEL, 2 * B_INNER), D_MODEL ** -0.5)
    b_rel_bias = nrm(ks[16], (N_B_LAYERS, B_HEADS, 2 * REL_CLIP + 1), 0.1)
    b_w_out = nrm(ks[17], (N_B_LAYERS, B_INNER, D_MODEL), DEEPNORM_BETA * B_INNER ** -0.5)
    b_ln_g = 1.0 + nrm(ks[18], (N_B_LAYERS, D_MODEL), 0.02)
    b_ln_b = nrm(ks[19], (N_B_LAYERS, D_MODEL), 0.02)
    return {"x": x, "a_w_in": a_w_in, "a_b_gate": a_b_gate, "a_conv_w": a_conv_w,
            "a_conv_b": a_conv_b, "a_w_q": a_w_q, "a_w_k": a_w_k, "a_w_v": a_w_v,
            "a_gn_w": a_gn_w, "a_skip": a_skip, "a_w_out": a_w_out, "a_ln_g": a_ln_g,
            "a_ln_b": a_ln_b, "kv_w": kv_w, "b_w_in": b_w_in, "b_rel_bias": b_rel_bias,
            "b_w_out": b_w_out, "b_ln_g": b_ln_g, "b_ln_b": b_ln_b}


def reference(x, a_w_in, a_b_gate, a_conv_w, a_conv_b, a_w_q, a_w_k, a_w_v, a_gn_w, a_skip,
              a_w_out, a_ln_g, a_ln_b, kv_w, b_w_in, b_rel_bias, b_w_out, b_ln_g, b_ln_b):
    k_pad = None
    v_pad = None
    for layer in range(DEPTH):
        if layer < N_A_LAYERS:
            l = layer
            y = mlstm_layer(x, a_w_in[l], a_b_gate[l], a_conv_w[l], a_conv_b[l], a_w_q[l],
                            a_w_k[l], a_w_v[l], a_gn_w[l], a_skip[l], a_w_out[l])
            x = layer_norm(DEEPNORM_ALPHA * x + y, a_ln_g[l], a_ln_b[l])
            if layer == N_A_LAYERS - 1:
                k_pad, v_pad = shared_band_kv(x, kv_w)
        else:
            l = layer - N_A_LAYERS
            y = chunk_attn_layer(x, k_pad, v_pad, b_w_in[l], b_rel_bias[l], b_w_out[l])
            x = layer_norm(DEEPNORM_ALPHA * x + y, b_ln_g[l], b_ln_b[l])
    return x
```

```python
import numpy as np
from contextlib import ExitStack
import concourse.bass as bass
import concourse.mybir as mybir
from concourse.bass_utils import run_bass_kernel_spmd

F32 = mybir.dt.float32
BF16 = mybir.dt.bfloat16
I32 = mybir.dt.int32
AF = mybir.ActivationFunctionType
ALU = mybir.AluOpType
AX = mybir.AxisListType

SEM_CAP = 30000
BSTOP = ""
NPER = 3
NSPLIT = 4
D = 1024
S = 8192
DH = 512
NEG = -80.0
ALPHA = 8.0 ** 0.25
LN_EPS = 1e-5
GN_EPS = 1e-6
CW, CB, GNW, SK, BI, BF = 0, 16, 20, 24, 28, 29
NVEC = 32


class Res:
    __slots__ = ("name", "w", "rs")

    def __init__(self, name=""):
        self.name = name
        self.w = None
        self.rs = {}


class Prog:
    ENG = ("pe", "act", "dve", "pool", "sp")

    def __init__(self, nc, same_engine_sync=True):
        self.nc = nc
        self.q = {e: [] for e in self.ENG}
        self.seq = {e: 0 for e in self.ENG}
        self.waited = {e: {} for e in self.ENG}
        self.dma_cnt = {}
        self.ses = same_engine_sync
        self.last = {}
        self.tags = {}

    def op(self, eng, fn, reads=(), writes=(), dma=None, inc=None):
        if dma is None:
            s = self.seq[eng]
            self.seq[eng] = s + 1
            tok = (("E", eng, s // SEM_CAP), s % SEM_CAP + 1)
            incv = 1
        else:
            incv = 16 if inc is None else inc
            for pre in ("a1_0_", "a1_1_", "a2_0_", "a2_1_", "b0_", "b1_"):
                if dma.startswith(pre):
                    dma = dma[len(pre):]
            self.dma_cnt[dma] = self.dma_cnt.get(dma, 0) + incv
            tok = (("D", dma), self.dma_cnt[dma])
        self.last[tok[0]] = tok[1]
        need = {}
        for r in reads:
            if r.w is not None:
                self._need(eng, dma, need, r.w)
        for w in writes:
            if w.w is not None:
                self._need(eng, dma, need, w.w)
            for it in w.rs.items():
                self._need(eng, dma, need, it)
        for sk, v in need.items():
            self.waited[eng][sk] = v
        for r in reads:
            if r.rs.get(tok[0], 0) < tok[1]:
                r.rs[tok[0]] = tok[1]
        for w in writes:
            w.w = tok
            w.rs = {}
        self.q[eng].append((list(need.items()), fn, tok, incv))
        self.tags.setdefault(eng, []).append(getattr(self, 'tag', ''))
        return tok

    def _need(self, eng, dma, need, t):
        sk, v = t
        if dma is None and sk[0] == "E" and sk[1] == eng and (eng == "pe" or not self.ses):
            return
        if self.waited[eng].get(sk, 0) >= v:
            return
        if need.get(sk, 0) < v:
            need[sk] = v

    def wait_tokens(self, eng, toks):
        need = {}
        for t in toks:
            sk, v = t
            if self.waited[eng].get(sk, 0) >= v:
                continue
            if need.get(sk, 0) < v:
                need[sk] = v
        for sk, v in need.items():
            self.waited[eng][sk] = v
        self.q[eng].append((list(need.items()), None, None, 0))

    def barrier(self):
        toks = [(k, v) for k, v in self.last.items() if k != ("D", "cc")]
        for e in self.ENG:
            self.wait_tokens(e, toks)

    def emit(self):
        nc = self.nc
        keys = set()
        for e in self.ENG:
            for waits, fn, tok, incv in self.q[e]:
                if tok is not None:
                    keys.add(tok[0])
                for sk, v in waits:
                    keys.add(sk)
        keys = sorted(keys, key=str)
        with ExitStack() as st:
            sems = {}
            for i, k in enumerate(keys):
                sems[k] = st.enter_context(nc.semaphore("s%d" % i))
            block = st.enter_context(nc.Block())

            def mk(ename):
                def body(eng):
                    for waits, fn, tok, incv in self.q[ename]:
                        for sk, v in waits:
                            eng.wait_ge(sems[sk], v)
                        if fn is not None:
                            fn(eng).then_inc(sems[tok[0]], incv)
                return body
            block.tensor(mk("pe"))
            block.scalar(mk("act"))
            block.vector(mk("dve"))
            block.gpsimd(mk("pool"))
            block.sync(mk("sp"))
        return len(keys)

    def mm(self, out, lhsT, rhs, start, stop, r, w):
        self.op("pe", lambda e: e.matmul(out, lhsT, rhs, start=start, stop=stop), r, w)

    def tr(self, out, in_, ident, r, w):
        self.op("pe", lambda e: e.transpose(out, in_, ident), r, w)

    def act(self, out, in_, func, r, w, bias=None, scale=None):
        kw = {}
        if bias is not None:
            kw["bias"] = bias
        if scale is not None:
            kw["scale"] = scale
        self.op("act", lambda e: e.activation(out=out, in_=in_, func=func, **kw), r, w)

    def tt(self, eng, out, in0, in1, op, r, w):
        self.op(eng, lambda e: e.tensor_tensor(out=out, in0=in0, in1=in1, op=op), r, w)

    def ts(self, eng, out, in0, s1, s2, op0, op1, r, w):
        if s2 is None:
            self.op(eng, lambda e: e.tensor_scalar(out=out, in0=in0, scalar1=s1, scalar2=None, op0=op0), r, w)
        else:
            self.op(eng, lambda e: e.tensor_scalar(out=out, in0=in0, scalar1=s1, scalar2=s2, op0=op0, op1=op1), r, w)

    def stt(self, eng, out, in0, scalar, in1, op0, op1, r, w):
        self.op(eng, lambda e: e.scalar_tensor_tensor(out=out, in0=in0, scalar=scalar, in1=in1, op0=op0, op1=op1), r, w)

    def cp(self, eng, out, in_, r, w):
        if eng == "act":
            self.op(eng, lambda e: e.copy(out=out, in_=in_), r, w)
        else:
            self.op(eng, lambda e: e.tensor_copy(out=out, in_=in_), r, w)

    def memset(self, eng, ap, val, w):
        self.op(eng, lambda e: e.memset(ap, val), (), w)

    def dma(self, q, out, in_, r, w, key):
        return self.op(q, lambda e: e.dma_start(out=out, in_=in_), r, w, dma=key)


class Env:
    pass


def gather_piece(E, kind, j):
    src, dst, rs, rd = (E.hg_in, E.hg_all, E.r_hg_in, E.r_hg_all) if kind == "hg" else (E.xg_in, E.xg_all, E.r_xg_in, E.r_xg_all)
    E.p.op("pool", lambda e: e.collective_compute("AllGather", ALU.bypass, replica_groups=E.groups,
                                                  ins=[src[j].ap().opt()], outs=[dst[j].ap().opt()]),
           [rs[j]], [rd[j]], dma="cc", inc=1)


def build(nlayers=4, dbg=False, ses=True, mode="full", groups=None):
    nc = bass.Bass("TRN2", target_bir_lowering=False)
    E = Env()
    E.nc = nc
    E.dbg = dbg
    E.names = []
    E.groups = groups if groups is not None else [[0, 1, 2, 3], [4, 5, 6, 7]]
    if mode == "B":
        dbg = False

    def din(name, shape, dt=F32, need=True):
        if not need:
            return None
        E.names.append(name)
        return nc.dram_tensor(name, shape, dt, kind="ExternalInput")
    a1only = bool(dbg and dbg >= 10)
    nA = min(nlayers, 2) if not a1only else 1
    needB = nlayers > 2
    if mode == "B":
        nA = 0
        needB = True
        nlayers = nlayers + 2
    E.xq0 = din("xq0", [2048, D])
    E.cinfo = din("cinfo", [1, 8], I32)
    E.c_mask = din("c_mask", [128, 128])
    E.c_ident = din("c_ident", [128, 128])
    E.a_win = [din("a_win%d" % l, [D, 1538], need=l < nA) for l in range(2)]
    E.a_wq = [din("a_wq%d" % l, [DH, DH], need=l < nA) for l in range(2)]
    E.a_wk = [din("a_wk%d" % l, [DH, DH], need=l < nA) for l in range(2)]
    E.a_wv = [din("a_wv%d" % l, [DH, DH], need=l < nA) for l in range(2)]
    E.a_vec = [din("a_vec%d" % l, [128, NVEC], need=l < nA) for l in range(2)]
    E.a_wout = [din("a_wout%d" % l, [2048, D], need=l < nA and not a1only) for l in range(2)]
    E.a_lng = [din("a_lng%d" % l, [128, D], need=l < nA and not a1only) for l in range(2)]
    E.a_lnb = [din("a_lnb%d" % l, [128, D], need=l < nA and not a1only) for l in range(2)]
    E.kv_w = din("kv_w", [D, 2048], need=needB)
    E.b_win = [din("b_win%d" % l, [D, 2048], need=l < nlayers - 2) for l in range(2)]
    E.b_wout = [din("b_wout%d" % l, [D, D], need=l < nlayers - 2) for l in range(2)]
    E.b_lng = [din("b_lng%d" % l, [128, D], need=l < nlayers - 2) for l in range(2)]
    E.b_lnb = [din("b_lnb%d" % l, [128, D], need=l < nlayers - 2) for l in range(2)]
    E.b_bias = [din("b_bias%d" % l, [128, 16 * 2 * 128], need=l < nlayers - 2) for l in range(2)]
    E.b_bconst = [din("b_bconst%d" % l, [128, 16], need=l < nlayers - 2) for l in range(2)]
    E.kvalid = din("kvalid", [128, 20 * 16], need=needB)
    E.out = nc.dram_tensor("out", [2048, D], F32, kind="ExternalOutput")
    E.hg_in = [nc.dram_tensor("hg_in%d" % j, [256, 2048], BF16) for j in range(8)]
    E.hg_all = [nc.dram_tensor("hg_all%d" % j, [1024, 2048], BF16) for j in range(8)]
    E.xg_in = [nc.dram_tensor("xg_in%d" % j, [D, 512], BF16) for j in range(4)]
    E.xg_all = [nc.dram_tensor("xg_all%d" % j, [4 * D, 512], BF16) for j in range(4)]
    E.park = [nc.dram_tensor("park%d" % i, [2048, D], F32) for i in range(2)]
    E.xTb = nc.dram_tensor("xTb", [D, 2048], BF16)
    E.r_hg_in = [Res() for _ in range(8)]; E.r_hg_all = [Res() for _ in range(8)]
    E.r_xg_in = [Res() for _ in range(4)]; E.r_xg_all = [Res() for _ in range(4)]
    E.r_park = [Res(), Res()]; E.r_xTb = Res()
    if dbg:
        E.dbg_hg = nc.dram_tensor("dbg_hg", [8, 256, 2048], BF16, kind="ExternalOutput")
        E.dbg_x = nc.dram_tensor("dbg_x", [2048, D], F32, kind="ExternalOutput")

    if "DP" in BSTOP:
        E.dbgP = nc.dram_tensor("dbgP", [4, 128, 640], BF16, kind="ExternalOutput")
        E.dbgS = nc.dram_tensor("dbgS", [4, 128, 640], F32, kind="ExternalOutput")
        E.r_dbgS = Res()
    p = Prog(nc, same_engine_sync=ses)
    E.p = p
    with ExitStack() as gst:
        sbg = lambda name, shape, dt: gst.enter_context(nc.sbuf_tensor(name, shape, dt))
        E.maskf = sbg("maskf", [128, 128], F32)
        E.identf = sbg("identf", [128, 128], F32)
        E.identb = sbg("identb", [128, 128], BF16)
        E.onesf = sbg("onesf", [128, 128], F32)
        E.onesb = sbg("onesb", [128, 4], BF16)
        E.r_const = Res("const")
        E.reg = gst.enter_context(nc.sync.register("roff"))
        E.reg4 = gst.enter_context(nc.sync.register("roff4"))

        def first(e):
            e.reg_load(E.reg, E.cinfo.ap()[0:1, 0:1])
            e.reg_load(E.reg4, E.cinfo.ap()[0:1, 4:5])
            return e.dma_start(out=E.maskf[:], in_=E.c_mask.ap())
        p.op("sp", first, (), [E.r_const], dma="cst")
        p.dma("sp", E.identf[:], E.c_ident.ap(), (), [E.r_const], "cst")
        p.dma("pool", E.identb[:], E.c_ident.ap(), (), [E.r_const], "cstb")
        p.memset("dve", E.onesf[:], 1.0, [E.r_const])
        p.memset("dve", E.onesb[:], 1.0, [E.r_const])
        if "DP" in BSTOP:
            E.dbgS_sb = sbg("dbgS_sb", [128, 640], F32)
        E.neghalf = sbg("neghalf", [128, 1], F32)
        p.memset("dve", E.neghalf[:], -0.5, [E.r_const])
        E.epsc = sbg("epsc", [128, 1], F32)
        p.memset("dve", E.epsc[:], LN_EPS, [E.r_const])
        last_tok = None
        if nlayers == 0:
            zt = sbg("zt", [128, D], F32); rz = Res()
            p.memset("dve", zt[:], 0.0, [rz])
            last_tok = [p.dma("sp", E.out.ap()[0:128, :], zt[:], [rz], (), "out")]
        if nlayers > 0:
            phase_pre(E)
        for l in range(nA):
            phase_A1(E, l)
            p.barrier()
            if dbg and dbg >= 10:
                break
            last_tok = phase_A2(E, l, final=(nlayers == l + 1))
            p.barrier()
        if nlayers > 2:
            last_tok = phase_B(E, nlayers - 2, pin0=(E.xq0 if mode == "B" else None))
        p.barrier()
        p.wait_tokens("sp", last_tok)
        nsem = p.emit()
    E.nsem = nsem
    return nc, E


def phase_pre(E):
    nc, p = E.nc, E.p
    with ExitStack() as st:
        sb = lambda name, shape, dt: st.enter_context(nc.sbuf_tensor(name, shape, dt))
        xb16 = [sb("pre_xb%d" % i, [128, D], BF16) for i in range(2)]; r_xb = [Res(), Res()]
        xTst = [sb("pre_xT%d" % i, [128, 8, 128], BF16) for i in range(2)]; r_xT = [Res(), Res()]
        tps = st.enter_context(nc.psum_tensor("pre_tps", [128, 8, 128], BF16)); r_tps = Res()
        for tt_ in range(16):
            b2, g, s = tt_ % 2, tt_ // 4, tt_ % 4
            p.dma("pool", xb16[b2][:], E.xq0.ap()[tt_ * 128:(tt_ + 1) * 128, :], (), [r_xb[b2]], "prex%d" % b2)
            for k in range(8):
                p.tr(tps[:, k, :], xb16[b2][:, k * 128:(k + 1) * 128], E.identb[:], [r_xb[b2], E.r_const], [r_tps])
            p.cp("dve", xTst[b2][:], tps[:, :, :], [r_tps], [r_xT[b2]])
            p.dma("sp", E.xg_in[g].ap()[:, s * 128:(s + 1) * 128].rearrange("(k p) t -> p k t", p=128), xTst[b2][:],
                  [r_xT[b2]], [E.r_xg_in[g]], "xg%d" % g)
            if s == 3:
                gather_piece(E, "xg", g)
        p.barrier()


def phase_A1(E, l):
    nc, p = E.nc, E.p
    with ExitStack() as st:
        sb = lambda name, shape, dt: st.enter_context(nc.sbuf_tensor(name, shape, dt))
        ps = lambda name, shape, dt: st.enter_context(nc.psum_tensor(name, shape, dt))
        n = "a1_%d_" % l
        win = sb(n + "win", [128, 8, 1538], BF16); r_win = Res()
        wq = sb(n + "wq", [128, 4, DH], BF16); wk = sb(n + "wk", [128, 4, DH], BF16); wv = sb(n + "wv", [128, 4, DH], BF16)
        r_wq = Res(); r_wk = Res(); r_wv = Res()
        vec = sb(n + "vec", [128, NVEC], F32); r_vec = Res()
        negbf = sb(n + "negbf", [128, 1], F32)
        xt = [sb(n + "xt%d" % i, [128, 8, 512], BF16) for i in range(2)]; r_xt = [Res(), Res()]
        xm32 = sb(n + "xm32", [128, 4, 516], F32); r_xm = [Res() for _ in range(4)]; r_xmh = Res()
        xmb = sb(n + "xmb", [128, 4, 512], BF16); r_xmb = [Res() for _ in range(4)]
        sz = sb(n + "sz", [128, 4, 512], F32); r_sz = [Res() for _ in range(4)]
        xc32 = sb(n + "xc32", [128, 4, 512], F32); r_xc = [Res() for _ in range(4)]
        xcb = sb(n + "xcb", [128, 4, 512], BF16); r_xcb = [Res() for _ in range(4)]
        cacc = [sb(n + "cacc%d" % i, [128, 512], F32) for i in range(2)]; r_cacc = [Res(), Res()]
        A = [sb(n + "A%d" % i, [128, 4, 512], F32) for i in range(2)]; r_A = [[Res() for _ in range(4)] for _ in range(2)]
        Bt = [sb(n + "Bt%d" % i, [128, 4, 512], F32) for i in range(2)]; r_Bt = [[Res() for _ in range(4)] for _ in range(2)]
        qT = [sb(n + "qT%d" % i, [128, 4, 512], BF16) for i in range(2)]; r_qT = [[Res() for _ in range(4)] for _ in range(2)]
        kT = [sb(n + "kT%d" % i, [128, 4, 512], BF16) for i in range(2)]; r_kT = [[Res() for _ in range(4)] for _ in range(2)]
        v = [sb(n + "v%d" % i, [128, 4, 512], BF16) for i in range(2)]; r_v = [[Res() for _ in range(4)] for _ in range(2)]
        k2 = [sb(n + "k2%d" % i, [128, 4, 512], BF16) for i in range(2)]; r_k2 = [[Res() for _ in range(4)] for _ in range(2)]
        C = sb(n + "C", [128, 4, 512], F32); r_C = [Res() for _ in range(4)]
        Cb = sb(n + "Cb", [128, 4, 512], BF16); r_Cb = [Res() for _ in range(4)]
        nv = sb(n + "nv", [128, 4], F32); r_nv = Res()
        nb16 = sb(n + "nb16", [128, 4], BF16); r_nb16 = Res()
        swT = [sb(n + "swT%d" % i, [128, 128], BF16) for i in range(2)]; r_swT = [Res(), Res()]
        hn = [sb(n + "hn%d" % i, [128, 512], BF16) for i in range(2)]; r_hn = [Res(), Res()]
        gtmp = [sb(n + "gtmp%d" % i, [128, 4, 128], F32) for i in range(2)]; r_gtmp = [Res(), Res()]
        hgst = [sb(n + "hgst%d" % i, [128, 4, 512], BF16) for i in range(2)]; r_hgst = [Res(), Res()]
        sm = [sb(n + "sm%d" % i, [128, 32], F32) for i in range(2)]; r_sm = [Res(), Res()]
        G = sb(n + "G", [128, 4, 64], F32)
        wgs = sb(n + "wgs", [128, 64], F32)
        Mv = sb(n + "Mv", [128, 65], F32); MX = sb(n + "MX", [128, 64], F32); r_M = Res()
        r_gate = [Res() for _ in range(16)]
        gsm = sb(n + "gsm", [128, 64], F32)
        arg = sb(n + "arg", [128, 4, 4], F32)
        r_gs = Res()
        proj = [ps(n + "pp%d" % i, [128, 512], F32) for i in range(2)]; r_proj = [Res(), Res()]
        num = ps(n + "num", [128, 512], F32); r_num = Res()
        upd = [ps(n + "upd%d" % i, [128, 512], F32) for i in range(2)]; r_upd = [Res(), Res()]
        misc = ps(n + "misc", [128, 512], F32); r_scT = Res(); r_den = Res(); r_nupd = Res()
        hT = ps(n + "hT", [128, 4, 256], BF16); r_hT = Res()
        gps = ps(n + "gps", [128, 512], F32); r_gps = Res()

        p.dma("pool", win[:], E.a_win[l].ap().rearrange("(k p) n -> p k n", p=128), (), [r_win], n + "wa")
        p.dma("pool", wq[:], E.a_wq[l].ap().rearrange("(k p) n -> p k n", p=128), (), [r_wq], n + "wq")
        p.dma("pool", wk[:], E.a_wk[l].ap().rearrange("(k p) n -> p k n", p=128), (), [r_wk], n + "wk")
        p.dma("pool", wv[:], E.a_wv[l].ap().rearrange("(k p) n -> p k n", p=128), (), [r_wv], n + "wv")
        p.dma("sp", vec[:], E.a_vec[l].ap(), (), [r_vec], n + "vec")
        p.ts("dve", negbf[:, 0:1], vec[:, BF:BF + 1], -1.0, None, ALU.mult, None, [r_vec], [r_vec])
        p.memset("dve", C[:], 0.0, r_C)
        p.memset("pool", Cb[:], 0.0, r_Cb)
        p.memset("dve", nv[:], 0.0, [r_nv])
        p.memset("pool", nb16[:], 0.0, [r_nb16])
        p.memset("dve", Mv[:, 0:1], 0.0, [r_M])
        p.memset("pool", xm32[:, :, 0:3], 0.0, [r_xmh])

        def x_src(i):
            rk = i // 4
            return "sp", E.xg_all[i % 4].ap()[rk * D:(rk + 1) * D, :].rearrange("(k p) t -> p k t", p=128), [E.r_xg_all[i % 4]]

        def load_x(i):
            q, src, rd = x_src(i)
            p.dma(q, xt[i % 2][:], src, rd, [r_xt[i % 2]], n + "x%d" % (i % 2))

        pcnt = [0]

        def nextp():
            k = pcnt[0] % 2
            pcnt[0] += 1
            return proj[k], r_proj[k]

        def proj_gen(i):
            xb = i % 2
            buf = i % 2
            if i + 1 < 16:
                load_x(i + 1)
            X, rX = xt[xb], r_xt[xb]
            for j in range(4):
                for k in range(8):
                    p.mm(gps[:, 2 * j:2 * j + 2], X[:, k, j * 128:(j + 1) * 128], win[:, k, 1536:1538], k == 0, k == 7,
                         [rX, r_win], [r_gps])
            gv = gps[:, 0:8].rearrange("p (j t) -> p j t", t=2)
            ipre, ef, spl, cc, cums, tots, cmx = (gsm[:, 0:4], gsm[:, 4:8], gsm[:, 8:12], gsm[:, 12:16], gsm[:, 16:20],
                                                 gsm[:, 20:24], gsm[:, 24:28])
            cmaxT = gsm[0:4, 28:29]
            dg = gsm[0:4, 32:36]
            p.act(ipre, gv[:, :, 0], AF.Identity, [r_gps, r_vec], [r_gs], bias=vec[:, BI:BI + 1])
            p.act(ef, gv[:, :, 1], AF.Exp, [r_gps, r_vec], [r_gs], bias=negbf[:, 0:1], scale=-1.0)
            p.act(spl, ef, AF.Ln, [r_gs], [r_gs], bias=1.0)
            p.mm(gps[:, 16:20], E.maskf[:], spl, True, True, [r_gs, E.r_const], [r_gps])
            p.mm(gps[:, 24:28], E.onesf[:], spl, True, True, [r_gs, E.r_const], [r_gps])
            p.tt("dve", cc, gps[:, 16:20], ipre, ALU.add, [r_gps, r_gs], [r_gs])
            p.cp("dve", cums, gps[:, 16:20], [r_gps], [r_gs])
            p.cp("dve", tots, gps[:, 24:28], [r_gps], [r_gs])
            p.mm(gps[0:4, 128:256], cc, E.identf[:], True, True, [r_gs, E.r_const], [r_gps])
            p.op("dve", lambda e: e.reduce_max(out=cmaxT, in_=gps[0:4, 128:256], axis=AX.X), [r_gps], [r_gs])
            p.ts("dve", dg, E.identf[0:4, 0:4], cmaxT, None, ALU.mult, None, [r_gs, E.r_const], [r_gs])
            p.mm(gps[:, 32:36], E.onesf[0:4, :], dg, True, True, [r_gs, E.r_const], [r_gps])
            p.cp("dve", cmx, gps[:, 32:36], [r_gps], [r_gs])
            for j in range(4):
                ch = 4 * i + j
                p.tt("dve", MX[:, ch:ch + 1], Mv[:, ch:ch + 1], cmx[:, j:j + 1], ALU.max, [r_gs, r_M], [r_M])
                p.tt("dve", Mv[:, ch + 1:ch + 2], MX[:, ch:ch + 1], tots[:, j:j + 1], ALU.subtract, [r_gs, r_M], [r_M])
            c0 = 4 * i
            p.tt("dve", arg[:, 0, :], cc, Mv[:, c0:c0 + 4], ALU.subtract, [r_gs, r_M], [r_gs])
            p.tt("dve", arg[:, 1, :], cc, MX[:, c0:c0 + 4], ALU.subtract, [r_gs, r_M], [r_gs])
            p.tt("dve", arg[:, 2, :], Mv[:, c0:c0 + 4], MX[:, c0:c0 + 4], ALU.subtract, [r_gs, r_M], [r_gs])
            p.tt("dve", arg[:, 3, :], cums, Mv[:, c0:c0 + 4], ALU.subtract, [r_gs, r_M], [r_gs])
            p.act(G[:, :, c0:c0 + 4], arg[:], AF.Exp, [r_gs], [r_gate[i]])
            p.ts("dve", wgs[:, c0:c0 + 4], G[:, 1, c0:c0 + 4], DH ** -0.5, None, ALU.mult, None, [r_gate[i]], [r_gate[i]])
            yield
            if i > 0:
                p.cp("act", xm32[:, :, 0:3], xm32[:, :, 512:515], r_xm, [r_xmh])
            for fb in range(12):
                pb, rpb = nextp()
                for k in range(8):
                    p.mm(pb[:], win[:, k, fb * 128:(fb + 1) * 128], X[:, k, :], k == 0, k == 7, [rX, r_win], [rpb])
                if fb < 4:
                    p.act(xm32[:, fb, 3:515], pb[:], AF.Copy, [rpb], [r_xm[fb]])
                    p.act(xmb[:, fb, :], pb[:], AF.Copy, [rpb], [r_xmb[fb]])
                    yield
                elif fb < 8:
                    p.act(sz[:, fb - 4, :], pb[:], AF.Silu, [rpb], [r_sz[fb - 4]])
                    yield
                else:
                    b = fb - 8
                    p.act(A[buf][:, b, :], pb[:], AF.Sigmoid, [rpb], [r_A[buf][b]])
                    p.stt("dve", A[buf][:, b, :], A[buf][:, b, :], vec[:, GNW + b:GNW + b + 1], sz[:, b, :], ALU.mult, ALU.mult,
                          [r_A[buf][b], r_sz[b], r_vec], [r_A[buf][b]])
                yield
            for b in range(4):
                ca, rca = cacc[b % 2], r_cacc[b % 2]
                p.ts("dve", ca[:], xm32[:, b, 0:512], vec[:, CW + 4 * b:CW + 4 * b + 1], None, ALU.mult, None,
                     [r_xm[b], r_xmh, r_vec], [rca])
                for tap in range(1, 4):
                    p.stt("dve", ca[:], xm32[:, b, tap:tap + 512], vec[:, CW + 4 * b + tap:CW + 4 * b + tap + 1], ca[:],
                          ALU.mult, ALU.add, [r_xm[b], r_xmh, r_vec, rca], [rca])
                p.act(xc32[:, b, :], ca[:], AF.Silu, [rca, r_vec], [r_xc[b]], bias=vec[:, CB + b:CB + b + 1])
                p.act(xcb[:, b, :], ca[:], AF.Silu, [rca, r_vec], [r_xcb[b]], bias=vec[:, CB + b:CB + b + 1])
                p.stt("dve", Bt[buf][:, b, :], xc32[:, b, :], vec[:, SK + b:SK + b + 1], sz[:, b, :], ALU.mult, ALU.mult,
                      [r_xc[b], r_sz[b], r_vec], [r_Bt[buf][b]])
                yield
            for ob in range(4):
                pb, rpb = nextp()
                for k in range(4):
                    p.mm(pb[:], wq[:, k, ob * 128:(ob + 1) * 128], xcb[:, k, :], k == 0, k == 3, [r_wq] + r_xcb, [rpb])
                p.act(qT[buf][:, ob, :], pb[:], AF.Copy, [rpb], [r_qT[buf][ob]])
                yield
            for ob in range(4):
                pb, rpb = nextp()
                for k in range(4):
                    p.mm(pb[:], wk[:, k, ob * 128:(ob + 1) * 128], xcb[:, k, :], k == 0, k == 3, [r_wk] + r_xcb, [rpb])
                p.act(kT[buf][:, ob, :], pb[:], AF.Identity, [rpb], [r_kT[buf][ob]], scale=DH ** -0.5)
                yield
            for j in range(4):
                pb, rpb = nextp()
                for k in range(4):
                    p.mm(pb[:], xmb[:, k, j * 128:(j + 1) * 128], wv[:, k, :], k == 0, k == 3, [r_wv] + r_xmb, [rpb])
                p.act(v[buf][:, j, :], pb[:], AF.Copy, [rpb], [r_v[buf][j]])
                yield
            for j in range(4):
                ch = 4 * i + j
                pb, rpb = nextp()
                for k in range(4):
                    p.mm(pb[:], xcb[:, k, j * 128:(j + 1) * 128], wk[:, k, :], k == 0, k == 3, [r_wk] + r_xcb, [rpb])
                p.act(k2[buf][:, j, :], pb[:], AF.Identity, [rpb, r_gate[i]], [r_k2[buf][j]], scale=wgs[:, ch:ch + 1])
                yield
        def rec_gen(i):
            buf = i % 2
            for j in range(4):
                ch = 4 * i + j
                js = slice(j * 128, (j + 1) * 128)
                cb = ch % 2
                scT = misc[:, 0:128]
                den = misc[:, 128:129]
                nupd = misc[:, 132:136]
                for k in range(4):
                    p.mm(scT, kT[buf][:, k, js], qT[buf][:, k, js], k == 0, k == 3, [r_kT[buf][k], r_qT[buf][k]], [r_scT])
                p.stt("dve", swT[cb][:], scT, G[:, 0, ch:ch + 1], E.maskf[:], ALU.mult, ALU.mult,
                      [r_scT, r_gate[i], E.r_const], [r_swT[cb]])
                yield
                for k in range(4):
                    p.mm(num[:], qT[buf][:, k, js], Cb[:, k, :], k == 0, False, [r_qT[buf][k], r_Cb[k]], [r_num])
                p.mm(num[:], swT[cb][:], v[buf][:, j, :], False, True, [r_swT[cb], r_v[buf][j]], [r_num])
                for k in range(4):
                    p.mm(den, qT[buf][:, k, js], nb16[:, k:k + 1], k == 0, False, [r_qT[buf][k], r_nb16], [r_den])
                p.mm(den, swT[cb][:], E.onesb[:, 0:1], False, True, [r_swT[cb], E.r_const], [r_den])
                for k in range(4):
                    ub, rub = upd[k % 2], r_upd[k % 2]
                    p.mm(ub[:], k2[buf][:, j, k * 128:(k + 1) * 128], v[buf][:, j, :], True, True,
                         [r_k2[buf][j], r_v[buf][j]], [rub])
                    p.stt("dve", C[:, k, :], C[:, k, :], G[:, 2, ch:ch + 1], ub[:], ALU.mult, ALU.add,
                          [rub, r_C[k], r_gate[i]], [r_C[k]])
                    p.cp("act", Cb[:, k, :], C[:, k, :], [r_C[k]], [r_Cb[k]])
                    p.mm(nupd[:, k:k + 1], k2[buf][:, j, k * 128:(k + 1) * 128], E.onesb[:, 0:1], True, True,
                         [r_k2[buf][j], E.r_const], [r_nupd])
                yield
                s_, rs_ = sm[cb], r_sm[cb]
                dd, d2, rstd, nbias, mv, st6 = s_[:, 0:1], s_[:, 1:2], s_[:, 2:3], s_[:, 3:4], s_[:, 4:6], s_[:, 8:14]
                p.cp("dve", d2, den, [r_den], [rs_])
                p.stt("dve", dd, d2, -1.0, d2, ALU.mult, ALU.max, [rs_], [rs_])
                p.ts("dve", dd, dd, G[:, 3, ch:ch + 1], None, ALU.max, None, [rs_, r_gate[i]], [rs_])
                p.stt("dve", nv[:], nv[:], G[:, 2, ch:ch + 1], nupd, ALU.mult, ALU.add, [r_nupd, r_nv, r_gate[i]], [r_nv])
                p.cp("act", nb16[:], nv[:], [r_nv], [r_nb16])
                p.op("dve", lambda e, st6=st6: e.bn_stats(out=st6, in_=num[:]), [r_num], [rs_])
                p.op("dve", lambda e, mv=mv, st6=st6: e.bn_aggr(out=mv, in_=st6), [rs_], [rs_])
                p.ts("dve", d2, dd, dd, GN_EPS, ALU.mult, ALU.mult, [rs_], [rs_])
                p.tt("dve", d2, d2, mv[:, 1:2], ALU.add, [rs_], [rs_])
                p.tt("pool", rstd, d2, E.neghalf[:, 0:1], ALU.pow, [rs_, E.r_const], [rs_])
                p.stt("dve", nbias, mv[:, 0:1], -1.0, rstd, ALU.mult, ALU.mult, [rs_], [rs_])
                p.act(hn[cb][:], num[:], AF.Identity, [r_num, rs_], [r_hn[cb]], bias=nbias, scale=rstd)
                for k in range(4):
                    p.tr(hT[:, k, 0:128], hn[cb][:, k * 128:(k + 1) * 128], E.identb[:], [r_hn[cb], E.r_const], [r_hT])
                p.tt("dve", gtmp[cb][:], hT[:, :, 0:128], A[buf][:, :, js], ALU.mult, [r_hT] + r_A[buf], [r_gtmp[cb]])
                p.tt("dve", hgst[buf][:, :, js], gtmp[cb][:], Bt[buf][:, :, js], ALU.add, [r_gtmp[cb]] + r_Bt[buf], [r_hgst[buf]])
                yield
            for fh in range(2):
                p.dma("sp", E.hg_in[(i % 4) * 2 + fh].ap()[:, (i // 4) * 512:(i // 4 + 1) * 512].rearrange("(k p) t -> p k t", p=128),
                      hgst[buf][:, 2 * fh:2 * fh + 2, :], [r_hgst[buf]], [E.r_hg_in[(i % 4) * 2 + fh]], "hgst%d" % ((i % 4) * 2 + fh))
            if i >= 12:
                for fh in range(2):
                    gather_piece(E, "hg", (i % 4) * 2 + fh)

        load_x(0)
        for _ in proj_gen(0):
            pass
        for i in range(16):
            rg = rec_gen(i)
            pg = proj_gen(i + 1) if i + 1 < 16 else iter(())
            alive = True
            for _ in rg:
                for _u in range(NPER):
                    if alive and next(pg, None) is None and False:
                        pass
                    if alive:
                        try:
                            next(pg)
                        except StopIteration:
                            alive = False
            for _ in pg:
                pass
        p.barrier()


def ln_tail(E, p, rr, r_rr, lng, lnb, r_ln, xn, r_xn, xnew, r_xnew, sm, r_sm, tsrc, r_tsrc, ident, tpv, r_tp, xTst, r_xTst, part=None):
    st12, mv, rstd, nbias = sm[:, 0:12], sm[:, 12:14], sm[:, 14:15], sm[:, 15:16]
    if part == "b":
        if tsrc is xnew:
            r_tsrc = r_xnew
        for k in range(8):
            p.tr(tpv[:, k, :], tsrc[:, k * 128:(k + 1) * 128], ident, [r_tsrc, E.r_const], [r_tp])
        p.cp("dve", xTst[:], tpv, [r_tp], [r_xTst])
        return
    p.op("dve", lambda e: e.bn_stats(out=st12[:, 0:6], in_=rr[:, 0:512]), [r_rr], [r_sm])
    p.op("dve", lambda e: e.bn_stats(out=st12[:, 6:12], in_=rr[:, 512:1024]), [r_rr], [r_sm])
    p.op("dve", lambda e: e.bn_aggr(out=mv, in_=st12), [r_sm], [r_sm])
    p.act(rstd, mv[:, 1:2], AF.Ln, [r_sm], [r_sm], bias=E.epsc[:, 0:1])
    p.act(rstd, rstd, AF.Exp, [r_sm], [r_sm], scale=-0.5)
    p.stt("dve", nbias, mv[:, 0:1], -1.0, rstd, ALU.mult, ALU.mult, [r_sm], [r_sm])
    p.act(xn[:], rr[:], AF.Identity, [r_rr, r_sm], [r_xn], bias=nbias, scale=rstd)
    p.tt("dve", xn[:], xn[:], lng[:], ALU.mult, [r_xn, r_ln], [r_xn])
    p.tt("dve", xnew[:], xn[:], lnb[:], ALU.add, [r_xn, r_ln], [r_xnew])
    if tsrc is not xnew:
        p.cp("act", tsrc[:], xnew[:], [r_xnew], [r_tsrc])
    else:
        r_tsrc = r_xnew
    if part == "a":
        return
    for k in range(8):
        p.tr(tpv[:, k, :], tsrc[:, k * 128:(k + 1) * 128], ident, [r_tsrc, E.r_const], [r_tp])
    p.cp("dve", xTst[:], tpv, [r_tp], [r_xTst])


def phase_A2(E, l, final):
    nc, p = E.nc, E.p
    toks = []
    with ExitStack() as st:
        sb = lambda name, shape, dt: st.enter_context(nc.sbuf_tensor(name, shape, dt))
        ps = lambda name, shape, dt: st.enter_context(nc.psum_tensor(name, shape, dt))
        n = "a2_%d_" % l
        wo = sb(n + "wo", [128, 16, D], BF16); r_wo = Res()
        lng = sb(n + "lng", [128, D], F32); lnb = sb(n + "lnb", [128, D], F32); r_ln = Res()
        hgt = [sb(n + "hgt%d" % i, [128, 2, 8, 512], BF16) for i in range(2)]; r_hgt = [Res(), Res()]
        xres = [sb(n + "xres%d" % i, [128, D], F32) for i in range(2)]; r_xres = [Res(), Res()]
        rr = [sb(n + "rr%d" % i, [128, D], F32) for i in range(2)]; r_rr = [Res(), Res()]
        xn = sb(n + "xn", [128, D], F32); r_xn = Res()
        xnew = [sb(n + "xnew%d" % i, [128, D], F32) for i in range(2)]; r_xnew = [Res(), Res()]
        xnb = sb(n + "xnb", [128, D], BF16); r_xnb = Res()
        xTst = [sb(n + "xTst%d" % i, [128, 8, 128], BF16) for i in range(2)]; r_xTst = [Res(), Res()]
        sm = [sb(n + "sm%d" % i, [128, 16], F32) for i in range(2)]; r_sm = [Res(), Res()]
        yps = [ps(n + "y%d" % i, [128, 2, 512], F32) for i in range(2)]; r_y = [Res(), Res()]
        tps = ps(n + "tps", [128, 8, 128], BF16); r_tps = Res()

        for hf in range(2):
            p.dma("pool", wo[:, hf * 8:(hf + 1) * 8, :], E.a_wout[l].ap()[hf * 1024:(hf + 1) * 1024, :].rearrange("(k p) n -> p k n", p=128),
                  (), [r_wo], n + "w")
        p.dma("sp", lng[:], E.a_lng[l].ap(), (), [r_ln], n + "ln")
        p.dma("sp", lnb[:], E.a_lnb[l].ap(), (), [r_ln], n + "ln")
        xsrc = E.xq0 if l == 0 else E.park[(l - 1) % 2]
        r_xsrc = () if l == 0 else [E.r_park[(l - 1) % 2]]
        pk = E.park[l % 2]
        def a2_load(g):
            hb = g % 2
            for fh in range(2):
                def ld(e, g=g, hb=hb, fh=fh):
                    return e.dma_start(out=hgt[hb][:, fh, :, :],
                                       in_=bass.AP(E.hg_all[2 * g + fh], E.reg, [[2048, 128], [128 * 2048, 8], [1, 512]]))
                p.op("sp", ld, [E.r_hg_all[2 * g + fh]], [r_hgt[hb]], dma=n + "hg%d" % hb)

        def a2_mm(tt_):
            g, s = divmod(tt_, 4)
            hb, b2 = g % 2, tt_ % 2
            ts_ = slice(s * 128, (s + 1) * 128)
            if s == 0 and g + 1 < 4:
                a2_load(g + 1)
            p.dma("sp", xres[b2][:], xsrc.ap()[tt_ * 128:(tt_ + 1) * 128, :], r_xsrc, [r_xres[b2]], n + "xr%d" % b2)
            for half in range(2):
                for k in range(16):
                    fh, kk = k // 8, k % 8
                    wk_ = (kk // 2) * 4 + fh * 2 + (kk % 2)
                    p.mm(yps[b2][:, half, :], hgt[hb][:, fh, kk, ts_], wo[:, wk_, half * 512:(half + 1) * 512], k == 0, k == 15,
                         [r_hgt[hb], r_wo], [r_y[b2]])

        def a2_tail(tt_):
            g, s = divmod(tt_, 4)
            b2 = tt_ % 2
            p.stt("dve", rr[b2][:], xres[b2][:], ALPHA, yps[b2][:].rearrange("p a b -> p (a b)"), ALU.mult, ALU.add,
                  [r_xres[b2], r_y[b2]], [r_rr[b2]])
            ln_tail(E, p, rr[b2], r_rr[b2], lng, lnb, r_ln, xn, r_xn, xnew[b2], r_xnew[b2], sm[b2], r_sm[b2],
                    xnb, r_xnb, E.identb[:], tps[:, :, :], r_tps, xTst[b2], r_xTst[b2])
            if final:
                toks.append(p.dma("sp", E.out.ap()[tt_ * 128:(tt_ + 1) * 128, :], xnew[b2][:], [r_xnew[b2]], (), n + "out"))
            else:
                p.dma("sp", pk.ap()[tt_ * 128:(tt_ + 1) * 128, :], xnew[b2][:], [r_xnew[b2]], [E.r_park[l % 2]], n + "pk")
                p.dma("sp", E.xg_in[g].ap()[:, s * 128:(s + 1) * 128].rearrange("(k p) t -> p k t", p=128), xTst[b2][:],
                      [r_xTst[b2]], [E.r_xg_in[g]], "xg%d" % g)
                if s == 3:
                    gather_piece(E, "xg", g)

        a2_load(0)
        a2_mm(0)
        for tt_ in range(16):
            if tt_ + 1 < 16:
                a2_mm(tt_ + 1)
            a2_tail(tt_)
        p.barrier()
    return toks


def phase_B(E, nb, pin0=None):
    nc, p = E.nc, E.p
    toks = []
    with ExitStack() as st:
        sb = lambda name, shape, dt: st.enter_context(nc.sbuf_tensor(name, shape, dt))
        KT = sb("KT", [128, 8, 2560], BF16); r_KT = [Res() for _ in range(5)]
        V = sb("V", [128, 20, 16, 66], BF16); r_V = [Res() for _ in range(20)]
        kval = sb("kval", [128, 20, 16], F32); r_kval = Res()
        p.dma("sp", kval[:].rearrange("p a b -> p (a b)"), E.kvalid.ap(), (), [r_kval], "kval")
        with ExitStack() as st2:
            sb2 = lambda name, shape, dt: st2.enter_context(nc.sbuf_tensor(name, shape, dt))
            ps2 = lambda name, shape, dt: st2.enter_context(nc.psum_tensor(name, shape, dt))
            wkv = sb2("wkv", [128, 8, 2048], BF16); r_wkv = Res()
            xh = [sb2("xh%d" % i, [128, 8, 512], BF16) for i in range(2)]; r_xh = [Res(), Res()]
            pp = [ps2("kvp%d" % i, [128, 512], F32) for i in range(2)]; r_pp = [Res(), Res()]
            p.dma("pool", wkv[:], E.kv_w.ap().rearrange("(k p) n -> p k n", p=128), (), [r_wkv], "wkv")
            cnt = 0
            for t in range(5):
                hb = t % 2
                if t == 0:
                    def ld(e, hb=hb):
                        return e.dma_start(out=xh[hb][:], in_=bass.AP(E.xg_all[3], E.reg4, [[512, 128], [128 * 512, 8], [1, 512]]))
                    p.op("sp", ld, [E.r_xg_all[3]], [r_xh[hb]], dma="xh%d" % hb)
                else:
                    p.dma("sp", xh[hb][:], E.xg_in[t - 1].ap().rearrange("(k p) t -> p k t", p=128),
                          [E.r_xg_in[t - 1]], [r_xh[hb]], "xh%d" % hb)
                for fb in range(8):
                    pb, rpb = pp[cnt % 2], r_pp[cnt % 2]; cnt += 1
                    for k in range(8):
                        p.mm(pb[:], wkv[:, k, fb * 128:(fb + 1) * 128], xh[hb][:, k, :], k == 0, k == 7, [r_wkv, r_xh[hb]], [rpb])
                    p.act(KT[:, fb, t * 512:(t + 1) * 512], pb[:], AF.Copy, [rpb], [r_KT[t]])
                for s in range(4):
                    kb = t * 4 + s
                    for half in range(2):
                        pb, rpb = pp[cnt % 2], r_pp[cnt % 2]; cnt += 1
                        for k in range(8):
                            p.mm(pb[:], xh[hb][:, k, s * 128:(s + 1) * 128], wkv[:, k, 1024 + half * 512:1024 + (half + 1) * 512],
                                 k == 0, k == 7, [r_wkv, r_xh[hb]], [rpb])
                        p.act(V[:, kb, half * 8:(half + 1) * 8, 0:64], pb[:].rearrange("p (h d) -> p h d", d=64), AF.Identity,
                              [rpb, r_kval], [r_V[kb]], scale=kval[:, kb, 0:1])
                    p.cp("pool", V[:, kb, :, 64:65], kval[:, kb, :].rearrange("p (h o) -> p h o", o=1), [r_kval], [r_V[kb]])
                    p.cp("pool", V[:, kb, :, 65:66], kval[:, kb, :].rearrange("p (h o) -> p h o", o=1), [r_kval], [r_V[kb]])
            p.barrier()
        if "kv" in BSTOP:
            toks.append(p.dma("sp", E.out.ap()[0:128, :], E.xq0.ap()[0:128, :], (), (), "out"))
            return toks
        for l in range(nb):
            final = (l == nb - 1)
            with ExitStack() as st2:
                sb2 = lambda name, shape, dt: st2.enter_context(nc.sbuf_tensor(name, shape, dt))
                ps2 = lambda name, shape, dt: st2.enter_context(nc.psum_tensor(name, shape, dt))
                n = "b%d_" % l
                wi = sb2(n + "wi", [128, 8, 2048], BF16); r_wi = Res()
                wo = sb2(n + "wo", [128, 8, D], BF16); r_wo = Res()
                lng = sb2(n + "lng", [128, D], F32); lnb = sb2(n + "lnb", [128, D], F32); r_ln = Res()
                bias = sb2(n + "bias", [128, 16, 2, 128], F32); bcon = sb2(n + "bcon", [128, 16], F32); r_bias = Res()
                xt = sb2(n + "xt", [128, 8, 512], BF16); r_xt = Res()
                QT = sb2(n + "QT", [128, 8, 512], BF16); r_QT = Res()
                sg = sb2(n + "sg", [128, D], F32); r_sg = Res()
                stmp = [sb2(n + "stmp%d" % i, [128, 256], F32) for i in range(2)]; r_stmp = [Res(), Res()]
                PT = [sb2(n + "PT%d" % i, [128, 5, 128], BF16) for i in range(2)]; r_PT = [Res(), Res()]
                rinv = [sb2(n + "rinv%d" % i, [128, 4], F32) for i in range(2)]; r_rinv = [Res(), Res()]
                og = sb2(n + "og", [128, D], F32); r_og = Res()
                ogT = sb2(n + "ogT", [128, 8, 128], BF16); r_ogT = Res()
                xres = sb2(n + "xres", [128, D], F32); r_xres = Res()
                rr = sb2(n + "rr", [128, D], F32); r_rr = Res()
                xn = sb2(n + "xn", [128, D], F32); r_xn = Res()
                xnew = sb2(n + "xnew", [128, D], F32); r_xnew = Res()
                xTst = sb2(n + "xTst", [128, 8, 128], BF16); r_xTst = Res()
                sm = sb2(n + "sm", [128, 16], F32); r_sm = Res()
                sps = [ps2(n + "sps%d" % i, [128, 1024], F32) for i in range(2)]; r_sps = [Res(), Res()]
                ops_ = [ps2(n + "ops%d" % i, [128, 512], F32) for i in range(2)]; r_ops = [Res(), Res()]
                pp = [ps2(n + "pp%d" % i, [128, 512], F32) for i in range(2)]; r_pp = [Res(), Res()]

                p.dma("pool", wi[:], E.b_win[l].ap().rearrange("(k p) n -> p k n", p=128), (), [r_wi], n + "wi")
                p.dma("pool", wo[:], E.b_wout[l].ap().rearrange("(k p) n -> p k n", p=128), (), [r_wo], n + "w")
                p.dma("sp", lng[:], E.b_lng[l].ap(), (), [r_ln], n + "ln")
                p.dma("sp", lnb[:], E.b_lnb[l].ap(), (), [r_ln], n + "ln")
                p.dma("sp", bias[:].rearrange("p a b c -> p (a b c)"), E.b_bias[l].ap(), (), [r_bias], n + "bias")
                p.dma("sp", bcon[:], E.b_bconst[l].ap(), (), [r_bias], n + "bias")
                bflat = bias[:].rearrange("p a b c -> p (a b c)")
                p.act(bflat, bflat, AF.Exp, [r_bias], [r_bias])

                pin, r_pin = E.park[(1 + l) % 2], E.r_park[(1 + l) % 2]
                if l == 0 and pin0 is not None:
                    pin, r_pin = pin0, Res()
                pout, r_pout = E.park[l % 2], E.r_park[l % 2]
                cnt = [0]

                def nextpp():
                    k = cnt[0] % 2
                    cnt[0] += 1
                    return pp[k], r_pp[k]
                tpv = sps[1][:, :].rearrange("p (k t) -> p k t", t=128)

                def q_proj(g):
                    xsrc_ap = E.xg_in[g].ap() if l == 0 else E.xTb.ap()[:, g * 512:(g + 1) * 512]
                    p.dma("sp", xt[:], xsrc_ap.rearrange("(k p) t -> p k t", p=128), [E.r_xg_in[g] if l == 0 else E.r_xTb], [r_xt], n + "xt")
                    for fb in range(8):
                        pb, rpb = nextpp()
                        for k in range(8):
                            p.mm(pb[:], wi[:, k, fb * 128:(fb + 1) * 128], xt[:, k, :], k == 0, k == 7, [r_wi, r_xt], [rpb])
                        p.act(QT[:, fb, :], pb[:], AF.Identity, [rpb], [r_QT], scale=0.125)

                def emit_scores(qb, h):
                    ts_ = slice((qb % 4) * 128, (qb % 4 + 1) * 128)
                    fb, po = h // 2, (h % 2) * 64
                    sb_i = h % 2
                    sp_, rsp = sps[sb_i], r_sps[sb_i]
                    P_, rP = PT[sb_i], r_PT[sb_i]
                    for j in range(5):
                        kb = qb + j
                        p.mm(sp_[:, j * 128:(j + 1) * 128], KT[po:po + 64, fb, kb * 128:(kb + 1) * 128],
                             QT[po:po + 64, fb, ts_], True, True, [r_KT[kb // 4], r_QT], [rsp])
                    p.act(P_[:, 0:3, :], sp_[:, 0:384].rearrange("p (j t) -> p j t", t=128), AF.Exp, [rsp, r_bias], [rP],
                          bias=bcon[:, h:h + 1])
                    p.memset("dve", P_[0:64, 0, 64:128], 0.0, [rP])
                    p.act(stmp[sb_i][:, 0:128], sp_[:, 384:512], AF.Exp, [rsp], [r_stmp[sb_i]])
                    p.act(stmp[sb_i][:, 128:256], sp_[:, 512:640], AF.Exp, [rsp], [r_stmp[sb_i]])
                    p.tt("dve", P_[:, 3:5, :], stmp[sb_i][:].rearrange("p (j t) -> p j t", t=128), bias[:, h, :, :], ALU.mult,
                         [r_stmp[sb_i], r_bias], [rP])

                def emit_pv(qb, h):
                    hg4, hh = h // 4, h % 4
                    ob, rob = ops_[hg4 % 2], r_ops[hg4 % 2]
                    P_, rP = PT[h % 2], r_PT[h % 2]
                    for j in range(5):
                        kb = qb + j
                        p.mm(ob[:, hh * 66:(hh + 1) * 66], P_[:, j, :], V[:, kb, h, :], j == 0, j == 4, [rP, r_V[kb]], [rob])
                    if hh == 3:
                        obv = ob[:, 0:264].rearrange("p (h d) -> p h d", d=66)
                        ri, rri = rinv[hg4 % 2], r_rinv[hg4 % 2]
                        p.op("dve", lambda e, ri=ri, obv=obv: e.reciprocal(out=ri[:].rearrange("p (h o) -> p h o", o=1),
                                                                          in_=obv[:, :, 64:65]), [rob], [rri])
                        for h2 in range(4):
                            hx = hg4 * 4 + h2
                            p.stt("dve", og[:, hx * 64:(hx + 1) * 64], obv[:, h2, 0:64], ri[:, h2:h2 + 1], sg[:, hx * 64:(hx + 1) * 64],
                                  ALU.mult, ALU.mult, [rob, rri, r_sg], [r_og])

                def heads(qb, lo, hi):
                    for idx in range(lo, hi):
                        if idx < 16:
                            emit_scores(qb, idx)
                        if idx >= 1:
                            emit_pv(qb, idx - 1)

                def a_start(qb):
                    ts_ = slice((qb % 4) * 128, (qb % 4 + 1) * 128)
                    p.dma("sp", xres[:], pin.ap()[qb * 128:(qb + 1) * 128, :], [r_pin], [r_xres], n + "xr")
                    for half in range(2):
                        pb, rpb = nextpp()
                        for k in range(8):
                            p.mm(pb[:], xt[:, k, ts_], wi[:, k, 1024 + half * 512:1024 + (half + 1) * 512], k == 0, k == 7,
                                 [r_wi, r_xt], [rpb])
                        p.act(sg[:, half * 512:(half + 1) * 512], pb[:], AF.Silu, [rpb], [r_sg])
                    heads(qb, 0, NSPLIT)

                def tail1(qb):
                    for k in range(8):
                        p.tr(tpv[:, k, :], og[:, k * 128:(k + 1) * 128], E.identf[:], [r_og, E.r_const], [r_sps[1]])
                    p.cp("act", ogT[:], tpv, [r_sps[1]], [r_ogT])
                    yb = sps[0]; ryb = r_sps[0]
                    for half in range(2):
                        for k in range(8):
                            p.mm(yb[:, half * 512:(half + 1) * 512], ogT[:, k, :], wo[:, k, half * 512:(half + 1) * 512], k == 0, k == 7,
                                 [r_ogT, r_wo], [ryb])
                    p.stt("dve", rr[:], xres[:], ALPHA, yb[:], ALU.mult, ALU.add, [r_xres, ryb], [r_rr])
                    ln_tail(E, p, rr, r_rr, lng, lnb, r_ln, xn, r_xn, xnew, r_xnew, sm, r_sm,
                            xnew, r_xnew, E.identf[:], tpv, r_sps[1], xTst, r_xTst, part="a")

                def tail2(qb):
                    ln_tail(E, p, rr, r_rr, lng, lnb, r_ln, xn, r_xn, xnew, r_xnew, sm, r_sm,
                            xnew, r_xnew, E.identf[:], tpv, r_sps[1], xTst, r_xTst, part="b")
                    if final:
                        toks.append(p.dma("sp", E.out.ap()[qb * 128:(qb + 1) * 128, :], xnew[:], [r_xnew], (), n + "out"))
                    else:
                        p.dma("sp", pout.ap()[qb * 128:(qb + 1) * 128, :], xnew[:], [r_xnew], [r_pout], n + "pk")
                        p.dma("sp", E.xTb.ap()[:, qb * 128:(qb + 1) * 128].rearrange("(k p) t -> p k t", p=128), xTst[:],
                              [r_xTst], [E.r_xTb], n + "xg")

                for qb in range(16):
                    if qb % 4 == 0:
                        q_proj(qb // 4)
                    a_start(qb)
                    if qb > 0:
                        tail2(qb - 1)
                    heads(qb, NSPLIT, 17)
                    tail1(qb)
                tail2(15)
                p.barrier()
    return toks


def _rep(vv):
    return np.ascontiguousarray(np.broadcast_to(np.asarray(vv, np.float32)[None, :], (128, vv.shape[0])))


def prep_inputs(inp, c):
    b, r = c // 4, c % 4
    f = lambda a: np.ascontiguousarray(np.asarray(a, dtype=np.float32))
    m = {}
    x = np.asarray(inp["x"], np.float32)
    m["xq0"] = f(x[b, r * 2048:(r + 1) * 2048])
    ci = np.zeros((1, 8), np.int32)
    ci[0, 0] = r * 512
    ci[0, 4] = max(r - 1, 0) * D * 512
    m["cinfo"] = ci
    s = np.arange(128)
    m["c_mask"] = f((s[:, None] <= s[None, :]))
    m["c_ident"] = f(np.eye(128))
    h = r
    for l in range(2):
        w = np.asarray(inp["a_w_in"][l], np.float32)
        m["a_win%d" % l] = f(np.concatenate([w[:, h * 512:(h + 1) * 512], w[:, 2048 + h * 512:2048 + (h + 1) * 512],
                                             w[:, 4096 + h * 512:4096 + (h + 1) * 512], w[:, 6144 + h:6145 + h],
                                             w[:, 6148 + h:6149 + h]], axis=1))
        m["a_wq%d" % l] = f(inp["a_w_q"][l][h])
        m["a_wk%d" % l] = f(inp["a_w_k"][l][h])
        m["a_wv%d" % l] = f(inp["a_w_v"][l][h])
        vec = np.zeros((128, NVEC), np.float32)
        cw = np.asarray(inp["a_conv_w"][l], np.float32)[:, h * 512:(h + 1) * 512]
        for blk in range(4):
            for tap in range(4):
                vec[:, CW + 4 * blk + tap] = cw[tap, blk * 128:(blk + 1) * 128]
            vec[:, CB + blk] = np.asarray(inp["a_conv_b"][l], np.float32)[h * 512 + blk * 128:h * 512 + (blk + 1) * 128]
            vec[:, GNW + blk] = np.asarray(inp["a_gn_w"][l], np.float32)[h * 512 + blk * 128:h * 512 + (blk + 1) * 128]
            vec[:, SK + blk] = np.asarray(inp["a_skip"][l], np.float32)[h * 512 + blk * 128:h * 512 + (blk + 1) * 128]
        vec[:, BI] = np.asarray(inp["a_b_gate"][l], np.float32)[h]
        vec[:, BF] = np.asarray(inp["a_b_gate"][l], np.float32)[4 + h]
        m["a_vec%d" % l] = vec
        m["a_wout%d" % l] = f(inp["a_w_out"][l])
        m["a_lng%d" % l] = _rep(np.asarray(inp["a_ln_g"][l]))
        m["a_lnb%d" % l] = _rep(np.asarray(inp["a_ln_b"][l]))
    m["kv_w"] = f(inp["kv_w"])
    kk = np.arange(640)[:, None]
    tq = np.arange(128)[None, :]
    ci_ = tq // 64
    valid = (kk >= 64 * ci_) & (kk < 64 * ci_ + 576)
    idx = np.clip(tq + 512 - kk, -128, 128) + 128
    for l in range(2):
        m["b_win%d" % l] = f(inp["b_w_in"][l])
        m["b_wout%d" % l] = f(inp["b_w_out"][l])
        m["b_lng%d" % l] = _rep(np.asarray(inp["b_ln_g"][l]))
        m["b_lnb%d" % l] = _rep(np.asarray(inp["b_ln_b"][l]))
        rb = np.asarray(inp["b_rel_bias"][l], np.float32)
        bt = np.where(valid[None], rb[:, idx], np.float32(NEG)).astype(np.float32)
        bt = bt.reshape(16, 5, 128, 128)[:, 3:5]
        m["b_bias%d" % l] = f(bt.transpose(2, 0, 1, 3).reshape(128, 16 * 2 * 128))
        m["b_bconst%d" % l] = _rep(rb[:, 256])
    kx = np.arange(2560).reshape(20, 128).T
    kvd = ((kx + r * 2048 - 512) >= 0).astype(np.float32)
    m["kvalid"] = f(np.repeat(kvd[:, :, None], 16, axis=2).reshape(128, 320))
    return m


_CACHE = {}


def kernel(**inputs):
    if "nc" not in _CACHE:
        _CACHE["nc"] = build(4)
    nc, E = _CACHE["nc"]
    in_maps = [prep_inputs(inputs, c) for c in range(8)]
    res = run_bass_kernel_spmd(nc, in_maps, core_ids=list(range(8)))
    out = np.zeros((2, S, D), np.float32)
    for c in range(8):
        b, r = c // 4, c % 4
        out[b, r * 2048:(r + 1) * 2048] = np.asarray(res.results[c]["out"], np.float32)
    return out
```

```python
import numpy as np
from contextlib import ExitStack
import concourse.bass as bass
import concourse.mybir as mybir
from concourse.bass_utils import run_bass_kernel_spmd

F32 = mybir.dt.float32
BF16 = mybir.dt.bfloat16
I32 = mybir.dt.int32
AF = mybir.ActivationFunctionType
ALU = mybir.AluOpType
AX = mybir.AxisListType

SEM_CAP = 30000
BSTOP = ""
NPER = 3
NSPLIT = 6
D = 1024
S = 8192
DH = 512
NEG = -80.0
ALPHA = 8.0 ** 0.25
LN_EPS = 1e-5
GN_EPS = 1e-6
CW, CB, GNW, SK, BI, BF = 0, 16, 20, 24, 28, 29
NVEC = 32


class Res:
    __slots__ = ("name", "w", "rs")

    def __init__(self, name=""):
        self.name = name
        self.w = None
        self.rs = {}


class Prog:
    ENG = ("pe", "act", "dve", "pool", "sp")

    def __init__(self, nc, same_engine_sync=True):
        self.nc = nc
        self.q = {e: [] for e in self.ENG}
        self.seq = {e: 0 for e in self.ENG}
        self.waited = {e: {} for e in self.ENG}
        self.dma_cnt = {}
        self.ses = same_engine_sync
        self.last = {}
        self.tags = {}

    def op(self, eng, fn, reads=(), writes=(), dma=None, inc=None):
        if dma is None:
            s = self.seq[eng]
            self.seq[eng] = s + 1
            tok = (("E", eng, s // SEM_CAP), s % SEM_CAP + 1)
            incv = 1
        else:
            incv = 16 if inc is None else inc
            for pre in ("a1_0_", "a1_1_", "a2_0_", "a2_1_", "b0_", "b1_"):
                if dma.startswith(pre):
                    dma = dma[len(pre):]
            self.dma_cnt[dma] = self.dma_cnt.get(dma, 0) + incv
            tok = (("D", dma), self.dma_cnt[dma])
        self.last[tok[0]] = tok[1]
        need = {}
        for r in reads:
            if r.w is not None:
                self._need(eng, dma, need, r.w)
        for w in writes:
            if w.w is not None:
                self._need(eng, dma, need, w.w)
            for it in w.rs.items():
                self._need(eng, dma, need, it)
        for sk, v in need.items():
            self.waited[eng][sk] = v
        for r in reads:
            if r.rs.get(tok[0], 0) < tok[1]:
                r.rs[tok[0]] = tok[1]
        for w in writes:
            w.w = tok
            w.rs = {}
        self.q[eng].append((list(need.items()), fn, tok, incv))
        self.tags.setdefault(eng, []).append(getattr(self, 'tag', ''))
        return tok

    def _need(self, eng, dma, need, t):
        sk, v = t
        if dma is None and sk[0] == "E" and sk[1] == eng and (eng == "pe" or not self.ses):
            return
        if self.waited[eng].get(sk, 0) >= v:
            return
        if need.get(sk, 0) < v:
            need[sk] = v

    def wait_tokens(self, eng, toks):
        need = {}
        for t in toks:
            sk, v = t
            if self.waited[eng].get(sk, 0) >= v:
                continue
            if need.get(sk, 0) < v:
                need[sk] = v
        for sk, v in need.items():
            self.waited[eng][sk] = v
        self.q[eng].append((list(need.items()), None, None, 0))

    def barrier(self):
        toks = [(k, v) for k, v in self.last.items() if k != ("D", "cc")]
        for e in self.ENG:
            self.wait_tokens(e, toks)

    def emit(self):
        nc = self.nc
        keys = set()
        for e in self.ENG:
            for waits, fn, tok, incv in self.q[e]:
                if tok is not None:
                    keys.add(tok[0])
                for sk, v in waits:
                    keys.add(sk)
        keys = sorted(keys, key=str)
        with ExitStack() as st:
            sems = {}
            for i, k in enumerate(keys):
                sems[k] = st.enter_context(nc.semaphore("s%d" % i))
            block = st.enter_context(nc.Block())

            def mk(ename):
                def body(eng):
                    for waits, fn, tok, incv in self.q[ename]:
                        for sk, v in waits:
                            eng.wait_ge(sems[sk], v)
                        if fn is not None:
                            fn(eng).then_inc(sems[tok[0]], incv)
                return body
            block.tensor(mk("pe"))
            block.scalar(mk("act"))
            block.vector(mk("dve"))
            block.gpsimd(mk("pool"))
            block.sync(mk("sp"))
        return len(keys)

    def mm(self, out, lhsT, rhs, start, stop, r, w):
        self.op("pe", lambda e: e.matmul(out, lhsT, rhs, start=start, stop=stop), r, w)

    def tr(self, out, in_, ident, r, w):
        self.op("pe", lambda e: e.transpose(out, in_, ident), r, w)

    def act(self, out, in_, func, r, w, bias=None, scale=None):
        kw = {}
        if bias is not None:
            kw["bias"] = bias
        if scale is not None:
            kw["scale"] = scale
        self.op("act", lambda e: e.activation(out=out, in_=in_, func=func, **kw), r, w)

    def tt(self, eng, out, in0, in1, op, r, w):
        self.op(eng, lambda e: e.tensor_tensor(out=out, in0=in0, in1=in1, op=op), r, w)

    def ts(self, eng, out, in0, s1, s2, op0, op1, r, w):
        if s2 is None:
            self.op(eng, lambda e: e.tensor_scalar(out=out, in0=in0, scalar1=s1, scalar2=None, op0=op0), r, w)
        else:
            self.op(eng, lambda e: e.tensor_scalar(out=out, in0=in0, scalar1=s1, scalar2=s2, op0=op0, op1=op1), r, w)

    def stt(self, eng, out, in0, scalar, in1, op0, op1, r, w):
        self.op(eng, lambda e: e.scalar_tensor_tensor(out=out, in0=in0, scalar=scalar, in1=in1, op0=op0, op1=op1), r, w)

    def cp(self, eng, out, in_, r, w):
        if eng == "act":
            self.op(eng, lambda e: e.copy(out=out, in_=in_), r, w)
        else:
            self.op(eng, lambda e: e.tensor_copy(out=out, in_=in_), r, w)

    def memset(self, eng, ap, val, w):
        self.op(eng, lambda e: e.memset(ap, val), (), w)

    def dma(self, q, out, in_, r, w, key):
        return self.op(q, lambda e: e.dma_start(out=out, in_=in_), r, w, dma=key)


class Env:
    pass


def gather_piece(E, kind, j):
    src, dst, rs, rd = (E.hg_in, E.hg_all, E.r_hg_in, E.r_hg_all) if kind == "hg" else (E.xg_in, E.xg_all, E.r_xg_in, E.r_xg_all)
    E.p.op("pool", lambda e: e.collective_compute("AllGather", ALU.bypass, replica_groups=E.groups,
                                                  ins=[src[j].ap().opt()], outs=[dst[j].ap().opt()]),
           [rs[j]], [rd[j]], dma="cc", inc=1)


def build(nlayers=4, dbg=False, ses=True, mode="full", groups=None):
    nc = bass.Bass("TRN2", target_bir_lowering=False)
    E = Env()
    E.nc = nc
    E.dbg = dbg
    E.names = []
    E.groups = groups if groups is not None else [[0, 1, 2, 3], [4, 5, 6, 7]]
    if mode == "B":
        dbg = False

    def din(name, shape, dt=F32, need=True):
        if not need:
            return None
        E.names.append(name)
        return nc.dram_tensor(name, shape, dt, kind="ExternalInput")
    a1only = bool(dbg and dbg >= 10)
    nA = min(nlayers, 2) if not a1only else 1
    needB = nlayers > 2
    if mode == "B":
        nA = 0
        needB = True
        nlayers = nlayers + 2
    E.xq0 = din("xq0", [2048, D])
    E.cinfo = din("cinfo", [1, 8], I32)
    E.c_mask = din("c_mask", [128, 128])
    E.c_ident = din("c_ident", [128, 128])
    E.a_win = [din("a_win%d" % l, [D, 1538], need=l < nA) for l in range(2)]
    E.a_wq = [din("a_wq%d" % l, [DH, DH], need=l < nA) for l in range(2)]
    E.a_wk = [din("a_wk%d" % l, [DH, DH], need=l < nA) for l in range(2)]
    E.a_wv = [din("a_wv%d" % l, [DH, DH], need=l < nA) for l in range(2)]
    E.a_vec = [din("a_vec%d" % l, [128, NVEC], need=l < nA) for l in range(2)]
    E.a_wout = [din("a_wout%d" % l, [2048, D], need=l < nA and not a1only) for l in range(2)]
    E.a_lng = [din("a_lng%d" % l, [128, D], need=l < nA and not a1only) for l in range(2)]
    E.a_lnb = [din("a_lnb%d" % l, [128, D], need=l < nA and not a1only) for l in range(2)]
    E.kv_w = din("kv_w", [D, 2048], need=needB)
    E.b_win = [din("b_win%d" % l, [D, 2048], need=l < nlayers - 2) for l in range(2)]
    E.b_wout = [din("b_wout%d" % l, [D, D], need=l < nlayers - 2) for l in range(2)]
    E.b_lng = [din("b_lng%d" % l, [128, D], need=l < nlayers - 2) for l in range(2)]
    E.b_lnb = [din("b_lnb%d" % l, [128, D], need=l < nlayers - 2) for l in range(2)]
    E.b_bias = [din("b_bias%d" % l, [128, 16 * 2 * 128], need=l < nlayers - 2) for l in range(2)]
    E.b_bconst = [din("b_bconst%d" % l, [128, 16], need=l < nlayers - 2) for l in range(2)]
    E.kvalid = din("kvalid", [128, 20 * 16], need=needB)
    E.out = nc.dram_tensor("out", [2048, D], F32, kind="ExternalOutput")
    E.hg_in = [nc.dram_tensor("hg_in%d" % j, [256, 2048], BF16) for j in range(8)]
    E.hg_all = [nc.dram_tensor("hg_all%d" % j, [1024, 2048], BF16) for j in range(8)]
    E.xg_in = [nc.dram_tensor("xg_in%d" % j, [D, 512], BF16) for j in range(4)]
    E.xg_all = [nc.dram_tensor("xg_all%d" % j, [4 * D, 512], BF16) for j in range(4)]
    E.park = [nc.dram_tensor("park%d" % i, [2048, D], F32) for i in range(2)]
    E.xTb = nc.dram_tensor("xTb", [D, 2048], BF16)
    E.r_hg_in = [Res() for _ in range(8)]; E.r_hg_all = [Res() for _ in range(8)]
    E.r_xg_in = [Res() for _ in range(4)]; E.r_xg_all = [Res() for _ in range(4)]
    E.r_park = [Res(), Res()]; E.r_xTb = Res()
    if dbg:
        E.dbg_hg = nc.dram_tensor("dbg_hg", [8, 256, 2048], BF16, kind="ExternalOutput")
        E.dbg_x = nc.dram_tensor("dbg_x", [2048, D], F32, kind="ExternalOutput")

    if "DP" in BSTOP:
        E.dbgP = nc.dram_tensor("dbgP", [4, 128, 640], BF16, kind="ExternalOutput")
        E.dbgS = nc.dram_tensor("dbgS", [4, 128, 640], F32, kind="ExternalOutput")
        E.r_dbgS = Res()
    p = Prog(nc, same_engine_sync=ses)
    E.p = p
    with ExitStack() as gst:
        sbg = lambda name, shape, dt: gst.enter_context(nc.sbuf_tensor(name, shape, dt))
        E.maskf = sbg("maskf", [128, 128], F32)
        E.identf = sbg("identf", [128, 128], F32)
        E.identb = sbg("identb", [128, 128], BF16)
        E.onesf = sbg("onesf", [128, 128], F32)
        E.onesb = sbg("onesb", [128, 4], BF16)
        E.r_const = Res("const")
        E.reg = gst.enter_context(nc.sync.register("roff"))
        E.reg4 = gst.enter_context(nc.sync.register("roff4"))

        def first(e):
            e.reg_load(E.reg, E.cinfo.ap()[0:1, 0:1])
            e.reg_load(E.reg4, E.cinfo.ap()[0:1, 4:5])
            return e.dma_start(out=E.maskf[:], in_=E.c_mask.ap())
        p.op("sp", first, (), [E.r_const], dma="cst")
        p.dma("sp", E.identf[:], E.c_ident.ap(), (), [E.r_const], "cst")
        p.dma("pool", E.identb[:], E.c_ident.ap(), (), [E.r_const], "cstb")
        p.memset("dve", E.onesf[:], 1.0, [E.r_const])
        p.memset("dve", E.onesb[:], 1.0, [E.r_const])
        if "DP" in BSTOP:
            E.dbgS_sb = sbg("dbgS_sb", [128, 640], F32)
        E.neghalf = sbg("neghalf", [128, 1], F32)
        p.memset("dve", E.neghalf[:], -0.5, [E.r_const])
        E.epsc = sbg("epsc", [128, 1], F32)
        p.memset("dve", E.epsc[:], LN_EPS, [E.r_const])
        last_tok = None
        if nlayers == 0:
            zt = sbg("zt", [128, D], F32); rz = Res()
            p.memset("dve", zt[:], 0.0, [rz])
            last_tok = [p.dma("sp", E.out.ap()[0:128, :], zt[:], [rz], (), "out")]
        if nlayers > 0:
            phase_pre(E)
        for l in range(nA):
            phase_A1(E, l)
            p.barrier()
            if dbg and dbg >= 10:
                break
            last_tok = phase_A2(E, l, final=(nlayers == l + 1))
            p.barrier()
        if nlayers > 2:
            last_tok = phase_B(E, nlayers - 2, pin0=(E.xq0 if mode == "B" else None))
        p.barrier()
        p.wait_tokens("sp", last_tok)
        nsem = p.emit()
    E.nsem = nsem
    return nc, E


def phase_pre(E):
    nc, p = E.nc, E.p
    with ExitStack() as st:
        sb = lambda name, shape, dt: st.enter_context(nc.sbuf_tensor(name, shape, dt))
        xb16 = [sb("pre_xb%d" % i, [128, D], BF16) for i in range(2)]; r_xb = [Res(), Res()]
        xTst = [sb("pre_xT%d" % i, [128, 8, 128], BF16) for i in range(2)]; r_xT = [Res(), Res()]
        tps = st.enter_context(nc.psum_tensor("pre_tps", [128, 8, 128], BF16)); r_tps = Res()
        for tt_ in range(16):
            b2, g, s = tt_ % 2, tt_ // 4, tt_ % 4
            p.dma("pool", xb16[b2][:], E.xq0.ap()[tt_ * 128:(tt_ + 1) * 128, :], (), [r_xb[b2]], "prex%d" % b2)
            for k in range(8):
                p.tr(tps[:, k, :], xb16[b2][:, k * 128:(k + 1) * 128], E.identb[:], [r_xb[b2], E.r_const], [r_tps])
            p.cp("dve", xTst[b2][:], tps[:, :, :], [r_tps], [r_xT[b2]])
            p.dma("sp", E.xg_in[g].ap()[:, s * 128:(s + 1) * 128].rearrange("(k p) t -> p k t", p=128), xTst[b2][:],
                  [r_xT[b2]], [E.r_xg_in[g]], "xg%d" % g)
            if s == 3:
                gather_piece(E, "xg", g)
        p.barrier()


def phase_A1(E, l):
    nc, p = E.nc, E.p
    with ExitStack() as st:
        sb = lambda name, shape, dt: st.enter_context(nc.sbuf_tensor(name, shape, dt))
        ps = lambda name, shape, dt: st.enter_context(nc.psum_tensor(name, shape, dt))
        n = "a1_%d_" % l
        win = sb(n + "win", [128, 8, 1538], BF16); r_win = Res()
        wq = sb(n + "wq", [128, 4, DH], BF16); wk = sb(n + "wk", [128, 4, DH], BF16); wv = sb(n + "wv", [128, 4, DH], BF16)
        r_wq = Res(); r_wk = Res(); r_wv = Res()
        vec = sb(n + "vec", [128, NVEC], F32); r_vec = Res()
        negbf = sb(n + "negbf", [128, 1], F32)
        xt = [sb(n + "xt%d" % i, [128, 8, 512], BF16) for i in range(2)]; r_xt = [Res(), Res()]
        xm32 = sb(n + "xm32", [128, 4, 516], F32); r_xm = [Res() for _ in range(4)]; r_xmh = Res()
        xmb = sb(n + "xmb", [128, 4, 512], BF16); r_xmb = [Res() for _ in range(4)]
        sz = sb(n + "sz", [128, 4, 512], F32); r_sz = [Res() for _ in range(4)]
        xc32 = sb(n + "xc32", [128, 4, 512], F32); r_xc = [Res() for _ in range(4)]
        xcb = sb(n + "xcb", [128, 4, 512], BF16); r_xcb = [Res() for _ in range(4)]
        cacc = [sb(n + "cacc%d" % i, [128, 512], F32) for i in range(2)]; r_cacc = [Res(), Res()]
        A = [sb(n + "A%d" % i, [128, 4, 512], F32) for i in range(2)]; r_A = [[Res() for _ in range(4)] for _ in range(2)]
        Bt = [sb(n + "Bt%d" % i, [128, 4, 512], F32) for i in range(2)]; r_Bt = [[Res() for _ in range(4)] for _ in range(2)]
        qT = [sb(n + "qT%d" % i, [128, 4, 512], BF16) for i in range(2)]; r_qT = [[Res() for _ in range(4)] for _ in range(2)]
        kT = [sb(n + "kT%d" % i, [128, 4, 512], BF16) for i in range(2)]; r_kT = [[Res() for _ in range(4)] for _ in range(2)]
        v = [sb(n + "v%d" % i, [128, 4, 512], BF16) for i in range(2)]; r_v = [[Res() for _ in range(4)] for _ in range(2)]
        k2 = [sb(n + "k2%d" % i, [128, 4, 512], BF16) for i in range(2)]; r_k2 = [[Res() for _ in range(4)] for _ in range(2)]
        C = sb(n + "C", [128, 4, 512], F32); r_C = [Res() for _ in range(4)]
        Cb = sb(n + "Cb", [128, 4, 512], BF16); r_Cb = [Res() for _ in range(4)]
        nv = sb(n + "nv", [128, 4], F32); r_nv = Res()
        nb16 = sb(n + "nb16", [128, 4], BF16); r_nb16 = Res()
        swT = [sb(n + "swT%d" % i, [128, 128], BF16) for i in range(2)]; r_swT = [Res(), Res()]
        hn = [sb(n + "hn%d" % i, [128, 512], BF16) for i in range(2)]; r_hn = [Res(), Res()]
        gtmp = [sb(n + "gtmp%d" % i, [128, 4, 128], F32) for i in range(2)]; r_gtmp = [Res(), Res()]
        hgst = [sb(n + "hgst%d" % i, [128, 4, 512], BF16) for i in range(2)]; r_hgst = [Res(), Res()]
        sm = [sb(n + "sm%d" % i, [128, 32], F32) for i in range(2)]; r_sm = [Res(), Res()]
        G = sb(n + "G", [128, 4, 64], F32)
        wgs = sb(n + "wgs", [128, 64], F32)
        Mv = sb(n + "Mv", [128, 65], F32); MX = sb(n + "MX", [128, 64], F32); r_M = Res()
        r_gate = [Res() for _ in range(16)]
        gsm = sb(n + "gsm", [128, 64], F32)
        arg = sb(n + "arg", [128, 4, 4], F32)
        r_gs = Res()
        proj = [ps(n + "pp%d" % i, [128, 512], F32) for i in range(2)]; r_proj = [Res(), Res()]
        num = ps(n + "num", [128, 512], F32); r_num = Res()
        upd = [ps(n + "upd%d" % i, [128, 512], F32) for i in range(2)]; r_upd = [Res(), Res()]
        misc = ps(n + "misc", [128, 512], F32); r_scT = Res(); r_den = Res(); r_nupd = Res()
        hT = ps(n + "hT", [128, 4, 256], BF16); r_hT = Res()
        gps = ps(n + "gps", [128, 512], F32); r_gps = Res()

        p.dma("pool", win[:], E.a_win[l].ap().rearrange("(k p) n -> p k n", p=128), (), [r_win], n + "wa")
        p.dma("pool", wq[:], E.a_wq[l].ap().rearrange("(k p) n -> p k n", p=128), (), [r_wq], n + "wq")
        p.dma("pool", wk[:], E.a_wk[l].ap().rearrange("(k p) n -> p k n", p=128), (), [r_wk], n + "wk")
        p.dma("pool", wv[:], E.a_wv[l].ap().rearrange("(k p) n -> p k n", p=128), (), [r_wv], n + "wv")
        p.dma("sp", vec[:], E.a_vec[l].ap(), (), [r_vec], n + "vec")
        p.ts("dve", negbf[:, 0:1], vec[:, BF:BF + 1], -1.0, None, ALU.mult, None, [r_vec], [r_vec])
        p.memset("dve", C[:], 0.0, r_C)
        p.memset("pool", Cb[:], 0.0, r_Cb)
        p.memset("dve", nv[:], 0.0, [r_nv])
        p.memset("pool", nb16[:], 0.0, [r_nb16])
        p.memset("dve", Mv[:, 0:1], 0.0, [r_M])
        p.memset("pool", xm32[:, :, 0:3], 0.0, [r_xmh])

        def x_src(i):
            rk = i // 4
            return "sp", E.xg_all[i % 4].ap()[rk * D:(rk + 1) * D, :].rearrange("(k p) t -> p k t", p=128), [E.r_xg_all[i % 4]]

        def load_x(i):
            q, src, rd = x_src(i)
            p.dma(q, xt[i % 2][:], src, rd, [r_xt[i % 2]], n + "x%d" % (i % 2))

        pcnt = [0]

        def nextp():
            k = pcnt[0] % 2
            pcnt[0] += 1
            return proj[k], r_proj[k]

        def proj_gen(i):
            xb = i % 2
            buf = i % 2
            if i + 1 < 16:
                load_x(i + 1)
            X, rX = xt[xb], r_xt[xb]
            for j in range(4):
                for k in range(8):
                    p.mm(gps[:, 2 * j:2 * j + 2], X[:, k, j * 128:(j + 1) * 128], win[:, k, 1536:1538], k == 0, k == 7,
                         [rX, r_win], [r_gps])
            gv = gps[:, 0:8].rearrange("p (j t) -> p j t", t=2)
            ipre, ef, spl, cc, cums, tots, cmx = (gsm[:, 0:4], gsm[:, 4:8], gsm[:, 8:12], gsm[:, 12:16], gsm[:, 16:20],
                                                 gsm[:, 20:24], gsm[:, 24:28])
            cmaxT = gsm[0:4, 28:29]
            dg = gsm[0:4, 32:36]
            p.act(ipre, gv[:, :, 0], AF.Identity, [r_gps, r_vec], [r_gs], bias=vec[:, BI:BI + 1])
            p.act(ef, gv[:, :, 1], AF.Exp, [r_gps, r_vec], [r_gs], bias=negbf[:, 0:1], scale=-1.0)
            p.act(spl, ef, AF.Ln, [r_gs], [r_gs], bias=1.0)
            p.mm(gps[:, 16:20], E.maskf[:], spl, True, True, [r_gs, E.r_const], [r_gps])
            p.mm(gps[:, 24:28], E.onesf[:], spl, True, True, [r_gs, E.r_const], [r_gps])
            p.tt("dve", cc, gps[:, 16:20], ipre, ALU.add, [r_gps, r_gs], [r_gs])
            p.cp("dve", cums, gps[:, 16:20], [r_gps], [r_gs])
            p.cp("dve", tots, gps[:, 24:28], [r_gps], [r_gs])
            p.mm(gps[0:4, 128:256], cc, E.identf[:], True, True, [r_gs, E.r_const], [r_gps])
            p.op("dve", lambda e: e.reduce_max(out=cmaxT, in_=gps[0:4, 128:256], axis=AX.X), [r_gps], [r_gs])
            p.ts("dve", dg, E.identf[0:4, 0:4], cmaxT, None, ALU.mult, None, [r_gs, E.r_const], [r_gs])
            p.mm(gps[:, 32:36], E.onesf[0:4, :], dg, True, True, [r_gs, E.r_const], [r_gps])
            p.cp("dve", cmx, gps[:, 32:36], [r_gps], [r_gs])
            for j in range(4):
                ch = 4 * i + j
                p.tt("dve", MX[:, ch:ch + 1], Mv[:, ch:ch + 1], cmx[:, j:j + 1], ALU.max, [r_gs, r_M], [r_M])
                p.tt("dve", Mv[:, ch + 1:ch + 2], MX[:, ch:ch + 1], tots[:, j:j + 1], ALU.subtract, [r_gs, r_M], [r_M])
            c0 = 4 * i
            p.tt("dve", arg[:, 0, :], cc, Mv[:, c0:c0 + 4], ALU.subtract, [r_gs, r_M], [r_gs])
            p.tt("dve", arg[:, 1, :], cc, MX[:, c0:c0 + 4], ALU.subtract, [r_gs, r_M], [r_gs])
            p.tt("dve", arg[:, 2, :], Mv[:, c0:c0 + 4], MX[:, c0:c0 + 4], ALU.subtract, [r_gs, r_M], [r_gs])
            p.tt("dve", arg[:, 3, :], cums, Mv[:, c0:c0 + 4], ALU.subtract, [r_gs, r_M], [r_gs])
            p.act(G[:, :, c0:c0 + 4], arg[:], AF.Exp, [r_gs], [r_gate[i]])
            p.ts("dve", wgs[:, c0:c0 + 4], G[:, 1, c0:c0 + 4], DH ** -0.5, None, ALU.mult, None, [r_gate[i]], [r_gate[i]])
            yield
            if i > 0:
                p.cp("act", xm32[:, :, 0:3], xm32[:, :, 512:515], r_xm, [r_xmh])
            for fb in range(12):
                pb, rpb = nextp()
                for k in range(8):
                    p.mm(pb[:], win[:, k, fb * 128:(fb + 1) * 128], X[:, k, :], k == 0, k == 7, [rX, r_win], [rpb])
                if fb < 4:
                    p.act(xm32[:, fb, 3:515], pb[:], AF.Copy, [rpb], [r_xm[fb]])
                    p.act(xmb[:, fb, :], pb[:], AF.Copy, [rpb], [r_xmb[fb]])
                    yield
                elif fb < 8:
                    p.act(sz[:, fb - 4, :], pb[:], AF.Silu, [rpb], [r_sz[fb - 4]])
                    yield
                else:
                    b = fb - 8
                    p.act(A[buf][:, b, :], pb[:], AF.Sigmoid, [rpb], [r_A[buf][b]])
                    p.stt("dve", A[buf][:, b, :], A[buf][:, b, :], vec[:, GNW + b:GNW + b + 1], sz[:, b, :], ALU.mult, ALU.mult,
                          [r_A[buf][b], r_sz[b], r_vec], [r_A[buf][b]])
                yield
            for b in range(4):
                ca, rca = cacc[b % 2], r_cacc[b % 2]
                p.ts("dve", ca[:], xm32[:, b, 0:512], vec[:, CW + 4 * b:CW + 4 * b + 1], None, ALU.mult, None,
                     [r_xm[b], r_xmh, r_vec], [rca])
                for tap in range(1, 4):
                    p.stt("dve", ca[:], xm32[:, b, tap:tap + 512], vec[:, CW + 4 * b + tap:CW + 4 * b + tap + 1], ca[:],
                          ALU.mult, ALU.add, [r_xm[b], r_xmh, r_vec, rca], [rca])
                p.act(xc32[:, b, :], ca[:], AF.Silu, [rca, r_vec], [r_xc[b]], bias=vec[:, CB + b:CB + b + 1])
                p.act(xcb[:, b, :], ca[:], AF.Silu, [rca, r_vec], [r_xcb[b]], bias=vec[:, CB + b:CB + b + 1])
                p.stt("dve", Bt[buf][:, b, :], xc32[:, b, :], vec[:, SK + b:SK + b + 1], sz[:, b, :], ALU.mult, ALU.mult,
                      [r_xc[b], r_sz[b], r_vec], [r_Bt[buf][b]])
                yield
            for ob in range(4):
                pb, rpb = nextp()
                for k in range(4):
                    p.mm(pb[:], wq[:, k, ob * 128:(ob + 1) * 128], xcb[:, k, :], k == 0, k == 3, [r_wq] + r_xcb, [rpb])
                p.act(qT[buf][:, ob, :], pb[:], AF.Copy, [rpb], [r_qT[buf][ob]])
                yield
            for ob in range(4):
                pb, rpb = nextp()
                for k in range(4):
                    p.mm(pb[:], wk[:, k, ob * 128:(ob + 1) * 128], xcb[:, k, :], k == 0, k == 3, [r_wk] + r_xcb, [rpb])
                p.act(kT[buf][:, ob, :], pb[:], AF.Identity, [rpb], [r_kT[buf][ob]], scale=DH ** -0.5)
                yield
            for j in range(4):
                pb, rpb = nextp()
                for k in range(4):
                    p.mm(pb[:], xmb[:, k, j * 128:(j + 1) * 128], wv[:, k, :], k == 0, k == 3, [r_wv] + r_xmb, [rpb])
                p.act(v[buf][:, j, :], pb[:], AF.Copy, [rpb], [r_v[buf][j]])
                yield
            for j in range(4):
                ch = 4 * i + j
                pb, rpb = nextp()
                for k in range(4):
                    p.mm(pb[:], xcb[:, k, j * 128:(j + 1) * 128], wk[:, k, :], k == 0, k == 3, [r_wk] + r_xcb, [rpb])
                p.act(k2[buf][:, j, :], pb[:], AF.Identity, [rpb, r_gate[i]], [r_k2[buf][j]], scale=wgs[:, ch:ch + 1])
                yield
        def rec_gen(i):
            buf = i % 2
            for j in range(4):
                ch = 4 * i + j
                js = slice(j * 128, (j + 1) * 128)
                cb = ch % 2
                scT = misc[:, 0:128]
                den = misc[:, 128:129]
                nupd = misc[:, 132:136]
                for k in range(4):
                    p.mm(scT, kT[buf][:, k, js], qT[buf][:, k, js], k == 0, k == 3, [r_kT[buf][k], r_qT[buf][k]], [r_scT])
                p.stt("dve", swT[cb][:], scT, G[:, 0, ch:ch + 1], E.maskf[:], ALU.mult, ALU.mult,
                      [r_scT, r_gate[i], E.r_const], [r_swT[cb]])
                yield
                for k in range(4):
                    p.mm(num[:], qT[buf][:, k, js], Cb[:, k, :], k == 0, False, [r_qT[buf][k], r_Cb[k]], [r_num])
                p.mm(num[:], swT[cb][:], v[buf][:, j, :], False, True, [r_swT[cb], r_v[buf][j]], [r_num])
                for k in range(4):
                    p.mm(den, qT[buf][:, k, js], nb16[:, k:k + 1], k == 0, False, [r_qT[buf][k], r_nb16], [r_den])
                p.mm(den, swT[cb][:], E.onesb[:, 0:1], False, True, [r_swT[cb], E.r_const], [r_den])
                for k in range(4):
                    ub, rub = upd[k % 2], r_upd[k % 2]
                    p.mm(ub[:], k2[buf][:, j, k * 128:(k + 1) * 128], v[buf][:, j, :], True, True,
                         [r_k2[buf][j], r_v[buf][j]], [rub])
                    p.stt("dve", C[:, k, :], C[:, k, :], G[:, 2, ch:ch + 1], ub[:], ALU.mult, ALU.add,
                          [rub, r_C[k], r_gate[i]], [r_C[k]])
                    p.cp("act", Cb[:, k, :], C[:, k, :], [r_C[k]], [r_Cb[k]])
                    p.mm(nupd[:, k:k + 1], k2[buf][:, j, k * 128:(k + 1) * 128], E.onesb[:, 0:1], True, True,
                         [r_k2[buf][j], E.r_const], [r_nupd])
                yield
                s_, rs_ = sm[cb], r_sm[cb]
                dd, d2, rstd, nbias, mv, st6 = s_[:, 0:1], s_[:, 1:2], s_[:, 2:3], s_[:, 3:4], s_[:, 4:6], s_[:, 8:14]
                p.cp("dve", d2, den, [r_den], [rs_])
                p.stt("dve", dd, d2, -1.0, d2, ALU.mult, ALU.max, [rs_], [rs_])
                p.ts("dve", dd, dd, G[:, 3, ch:ch + 1], None, ALU.max, None, [rs_, r_gate[i]], [rs_])
                p.stt("dve", nv[:], nv[:], G[:, 2, ch:ch + 1], nupd, ALU.mult, ALU.add, [r_nupd, r_nv, r_gate[i]], [r_nv])
                p.cp("act", nb16[:], nv[:], [r_nv], [r_nb16])
                p.op("dve", lambda e, st6=st6: e.bn_stats(out=st6, in_=num[:]), [r_num], [rs_])
                p.op("dve", lambda e, mv=mv, st6=st6: e.bn_aggr(out=mv, in_=st6), [rs_], [rs_])
                p.ts("dve", d2, dd, dd, GN_EPS, ALU.mult, ALU.mult, [rs_], [rs_])
                p.tt("dve", d2, d2, mv[:, 1:2], ALU.add, [rs_], [rs_])
                p.tt("pool", rstd, d2, E.neghalf[:, 0:1], ALU.pow, [rs_, E.r_const], [rs_])
                p.stt("dve", nbias, mv[:, 0:1], -1.0, rstd, ALU.mult, ALU.mult, [rs_], [rs_])
                p.act(hn[cb][:], num[:], AF.Identity, [r_num, rs_], [r_hn[cb]], bias=nbias, scale=rstd)
                for k in range(4):
                    p.tr(hT[:, k, 0:128], hn[cb][:, k * 128:(k + 1) * 128], E.identb[:], [r_hn[cb], E.r_const], [r_hT])
                p.tt("dve", gtmp[cb][:], hT[:, :, 0:128], A[buf][:, :, js], ALU.mult, [r_hT] + r_A[buf], [r_gtmp[cb]])
                p.tt("dve", hgst[buf][:, :, js], gtmp[cb][:], Bt[buf][:, :, js], ALU.add, [r_gtmp[cb]] + r_Bt[buf], [r_hgst[buf]])
                yield
            for fh in range(2):
                p.dma("sp", E.hg_in[(i % 4) * 2 + fh].ap()[:, (i // 4) * 512:(i // 4 + 1) * 512].rearrange("(k p) t -> p k t", p=128),
                      hgst[buf][:, 2 * fh:2 * fh + 2, :], [r_hgst[buf]], [E.r_hg_in[(i % 4) * 2 + fh]], "hgst%d" % ((i % 4) * 2 + fh))
            if i >= 12:
                for fh in range(2):
                    gather_piece(E, "hg", (i % 4) * 2 + fh)

        load_x(0)
        for _ in proj_gen(0):
            pass
        for i in range(16):
            rg = rec_gen(i)
            pg = proj_gen(i + 1) if i + 1 < 16 else iter(())
            alive = True
            for _ in rg:
                for _u in range(NPER):
                    if alive and next(pg, None) is None and False:
                        pass
                    if alive:
                        try:
                            next(pg)
                        except StopIteration:
                            alive = False
            for _ in pg:
                pass
        p.barrier()


def ln_tail(E, p, rr, r_rr, lng, lnb, r_ln, xn, r_xn, xnew, r_xnew, sm, r_sm, tsrc, r_tsrc, ident, tpv, r_tp, xTst, r_xTst, part=None):
    st12, mv, rstd, nbias = sm[:, 0:12], sm[:, 12:14], sm[:, 14:15], sm[:, 15:16]
    if part == "b":
        if tsrc is xnew:
            r_tsrc = r_xnew
        for k in range(8):
            p.tr(tpv[:, k, :], tsrc[:, k * 128:(k + 1) * 128], ident, [r_tsrc, E.r_const], [r_tp])
        p.cp("dve", xTst[:], tpv, [r_tp], [r_xTst])
        return
    p.op("dve", lambda e: e.bn_stats(out=st12[:, 0:6], in_=rr[:, 0:512]), [r_rr], [r_sm])
    p.op("dve", lambda e: e.bn_stats(out=st12[:, 6:12], in_=rr[:, 512:1024]), [r_rr], [r_sm])
    p.op("dve", lambda e: e.bn_aggr(out=mv, in_=st12), [r_sm], [r_sm])
    p.act(rstd, mv[:, 1:2], AF.Ln, [r_sm], [r_sm], bias=E.epsc[:, 0:1])
    p.act(rstd, rstd, AF.Exp, [r_sm], [r_sm], scale=-0.5)
    p.stt("dve", nbias, mv[:, 0:1], -1.0, rstd, ALU.mult, ALU.mult, [r_sm], [r_sm])
    p.act(xn[:], rr[:], AF.Identity, [r_rr, r_sm], [r_xn], bias=nbias, scale=rstd)
    p.tt("dve", xn[:], xn[:], lng[:], ALU.mult, [r_xn, r_ln], [r_xn])
    p.tt("dve", xnew[:], xn[:], lnb[:], ALU.add, [r_xn, r_ln], [r_xnew])
    if tsrc is not xnew:
        p.cp("act", tsrc[:], xnew[:], [r_xnew], [r_tsrc])
    else:
        r_tsrc = r_xnew
    if part == "a":
        return
    for k in range(8):
        p.tr(tpv[:, k, :], tsrc[:, k * 128:(k + 1) * 128], ident, [r_tsrc, E.r_const], [r_tp])
    p.cp("dve", xTst[:], tpv, [r_tp], [r_xTst])


def phase_A2(E, l, final):
    nc, p = E.nc, E.p
    toks = []
    with ExitStack() as st:
        sb = lambda name, shape, dt: st.enter_context(nc.sbuf_tensor(name, shape, dt))
        ps = lambda name, shape, dt: st.enter_context(nc.psum_tensor(name, shape, dt))
        n = "a2_%d_" % l
        wo = sb(n + "wo", [128, 16, D], BF16); r_wo = Res()
        lng = sb(n + "lng", [128, D], F32); lnb = sb(n + "lnb", [128, D], F32); r_ln = Res()
        hgt = [sb(n + "hgt%d" % i, [128, 2, 8, 512], BF16) for i in range(2)]; r_hgt = [Res(), Res()]
        xres = [sb(n + "xres%d" % i, [128, D], F32) for i in range(2)]; r_xres = [Res(), Res()]
        rr = [sb(n + "rr%d" % i, [128, D], F32) for i in range(2)]; r_rr = [Res(), Res()]
        xn = sb(n + "xn", [128, D], F32); r_xn = Res()
        xnew = [sb(n + "xnew%d" % i, [128, D], F32) for i in range(2)]; r_xnew = [Res(), Res()]
        xnb = sb(n + "xnb", [128, D], BF16); r_xnb = Res()
        xTst = [sb(n + "xTst%d" % i, [128, 8, 128], BF16) for i in range(2)]; r_xTst = [Res(), Res()]
        sm = [sb(n + "sm%d" % i, [128, 16], F32) for i in range(2)]; r_sm = [Res(), Res()]
        yps = [ps(n + "y%d" % i, [128, 2, 512], F32) for i in range(2)]; r_y = [Res(), Res()]
        tps = ps(n + "tps", [128, 8, 128], BF16); r_tps = Res()

        for hf in range(2):
            p.dma("pool", wo[:, hf * 8:(hf + 1) * 8, :], E.a_wout[l].ap()[hf * 1024:(hf + 1) * 1024, :].rearrange("(k p) n -> p k n", p=128),
                  (), [r_wo], n + "w")
        p.dma("sp", lng[:], E.a_lng[l].ap(), (), [r_ln], n + "ln")
        p.dma("sp", lnb[:], E.a_lnb[l].ap(), (), [r_ln], n + "ln")
        xsrc = E.xq0 if l == 0 else E.park[(l - 1) % 2]
        r_xsrc = () if l == 0 else [E.r_park[(l - 1) % 2]]
        pk = E.park[l % 2]
        def a2_load(g):
            hb = g % 2
            for fh in range(2):
                def ld(e, g=g, hb=hb, fh=fh):
                    return e.dma_start(out=hgt[hb][:, fh, :, :],
                                       in_=bass.AP(E.hg_all[2 * g + fh], E.reg, [[2048, 128], [128 * 2048, 8], [1, 512]]))
                p.op("sp", ld, [E.r_hg_all[2 * g + fh]], [r_hgt[hb]], dma=n + "hg%d" % hb)

        def a2_mm(tt_):
            g, s = divmod(tt_, 4)
            hb, b2 = g % 2, tt_ % 2
            ts_ = slice(s * 128, (s + 1) * 128)
            if s == 0 and g + 1 < 4:
                a2_load(g + 1)
            p.dma("sp", xres[b2][:], xsrc.ap()[tt_ * 128:(tt_ + 1) * 128, :], r_xsrc, [r_xres[b2]], n + "xr%d" % b2)
            for half in range(2):
                for k in range(16):
                    fh, kk = k // 8, k % 8
                    wk_ = (kk // 2) * 4 + fh * 2 + (kk % 2)
                    p.mm(yps[b2][:, half, :], hgt[hb][:, fh, kk, ts_], wo[:, wk_, half * 512:(half + 1) * 512], k == 0, k == 15,
                         [r_hgt[hb], r_wo], [r_y[b2]])

        def a2_tail(tt_):
            g, s = divmod(tt_, 4)
            b2 = tt_ % 2
            p.stt("dve", rr[b2][:], xres[b2][:], ALPHA, yps[b2][:].rearrange("p a b -> p (a b)"), ALU.mult, ALU.add,
                  [r_xres[b2], r_y[b2]], [r_rr[b2]])
            ln_tail(E, p, rr[b2], r_rr[b2], lng, lnb, r_ln, xn, r_xn, xnew[b2], r_xnew[b2], sm[b2], r_sm[b2],
                    xnb, r_xnb, E.identb[:], tps[:, :, :], r_tps, xTst[b2], r_xTst[b2])
            if final:
                toks.append(p.dma("sp", E.out.ap()[tt_ * 128:(tt_ + 1) * 128, :], xnew[b2][:], [r_xnew[b2]], (), n + "out"))
            else:
                p.dma("sp", pk.ap()[tt_ * 128:(tt_ + 1) * 128, :], xnew[b2][:], [r_xnew[b2]], [E.r_park[l % 2]], n + "pk")
                p.dma("sp", E.xg_in[g].ap()[:, s * 128:(s + 1) * 128].rearrange("(k p) t -> p k t", p=128), xTst[b2][:],
                      [r_xTst[b2]], [E.r_xg_in[g]], "xg%d" % g)
                if s == 3:
                    gather_piece(E, "xg", g)

        a2_load(0)
        a2_mm(0)
        for tt_ in range(16):
            if tt_ + 1 < 16:
                a2_mm(tt_ + 1)
            a2_tail(tt_)
        p.barrier()
    return toks


def phase_B(E, nb, pin0=None):
    nc, p = E.nc, E.p
    toks = []
    with ExitStack() as st:
        sb = lambda name, shape, dt: st.enter_context(nc.sbuf_tensor(name, shape, dt))
        KT = sb("KT", [128, 8, 2560], BF16); r_KT = [Res() for _ in range(5)]
        V = sb("V", [128, 20, 16, 66], BF16); r_V = [Res() for _ in range(20)]
        kval = sb("kval", [128, 20, 16], F32); r_kval = Res()
        p.dma("sp", kval[:].rearrange("p a b -> p (a b)"), E.kvalid.ap(), (), [r_kval], "kval")
        with ExitStack() as st2:
            sb2 = lambda name, shape, dt: st2.enter_context(nc.sbuf_tensor(name, shape, dt))
            ps2 = lambda name, shape, dt: st2.enter_context(nc.psum_tensor(name, shape, dt))
            wkv = sb2("wkv", [128, 8, 2048], BF16); r_wkv = Res()
            xh = [sb2("xh%d" % i, [128, 8, 512], BF16) for i in range(2)]; r_xh = [Res(), Res()]
            pp = [ps2("kvp%d" % i, [128, 512], F32) for i in range(2)]; r_pp = [Res(), Res()]
            p.dma("pool", wkv[:], E.kv_w.ap().rearrange("(k p) n -> p k n", p=128), (), [r_wkv], "wkv")
            cnt = 0
            for t in range(5):
                hb = t % 2
                if t == 0:
                    def ld(e, hb=hb):
                        return e.dma_start(out=xh[hb][:], in_=bass.AP(E.xg_all[3], E.reg4, [[512, 128], [128 * 512, 8], [1, 512]]))
                    p.op("sp", ld, [E.r_xg_all[3]], [r_xh[hb]], dma="xh%d" % hb)
                else:
                    p.dma("sp", xh[hb][:], E.xg_in[t - 1].ap().rearrange("(k p) t -> p k t", p=128),
                          [E.r_xg_in[t - 1]], [r_xh[hb]], "xh%d" % hb)
                for fb in range(8):
                    pb, rpb = pp[cnt % 2], r_pp[cnt % 2]; cnt += 1
                    for k in range(8):
                        p.mm(pb[:], wkv[:, k, fb * 128:(fb + 1) * 128], xh[hb][:, k, :], k == 0, k == 7, [r_wkv, r_xh[hb]], [rpb])
                    p.act(KT[:, fb, t * 512:(t + 1) * 512], pb[:], AF.Copy, [rpb], [r_KT[t]])
                for s in range(4):
                    kb = t * 4 + s
                    for half in range(2):
                        pb, rpb = pp[cnt % 2], r_pp[cnt % 2]; cnt += 1
                        for k in range(8):
                            p.mm(pb[:], xh[hb][:, k, s * 128:(s + 1) * 128], wkv[:, k, 1024 + half * 512:1024 + (half + 1) * 512],
                                 k == 0, k == 7, [r_wkv, r_xh[hb]], [rpb])
                        p.act(V[:, kb, half * 8:(half + 1) * 8, 0:64], pb[:].rearrange("p (h d) -> p h d", d=64), AF.Identity,
                              [rpb, r_kval], [r_V[kb]], scale=kval[:, kb, 0:1])
                    p.cp("pool", V[:, kb, :, 64:65], kval[:, kb, :].rearrange("p (h o) -> p h o", o=1), [r_kval], [r_V[kb]])
                    p.cp("pool", V[:, kb, :, 65:66], kval[:, kb, :].rearrange("p (h o) -> p h o", o=1), [r_kval], [r_V[kb]])
            p.barrier()
        if "kv" in BSTOP:
            toks.append(p.dma("sp", E.out.ap()[0:128, :], E.xq0.ap()[0:128, :], (), (), "out"))
            return toks
        for l in range(nb):
            final = (l == nb - 1)
            with ExitStack() as st2:
                sb2 = lambda name, shape, dt: st2.enter_context(nc.sbuf_tensor(name, shape, dt))
                ps2 = lambda name, shape, dt: st2.enter_context(nc.psum_tensor(name, shape, dt))
                n = "b%d_" % l
                wi = sb2(n + "wi", [128, 8, 2048], BF16); r_wi = Res()
                wo = sb2(n + "wo", [128, 8, D], BF16); r_wo = Res()
                lng = sb2(n + "lng", [128, D], F32); lnb = sb2(n + "lnb", [128, D], F32); r_ln = Res()
                bias = sb2(n + "bias", [128, 16, 2, 128], F32); bcon = sb2(n + "bcon", [128, 16], F32); r_bias = Res()
                xt = sb2(n + "xt", [128, 8, 512], BF16); r_xt = Res()
                QT = sb2(n + "QT", [128, 8, 512], BF16); r_QT = Res()
                sg = sb2(n + "sg", [128, D], F32); r_sg = Res()
                stmp = [sb2(n + "stmp%d" % i, [128, 256], F32) for i in range(2)]; r_stmp = [Res(), Res()]
                PT = [sb2(n + "PT%d" % i, [128, 5, 128], BF16) for i in range(2)]; r_PT = [Res(), Res()]
                rinv = [sb2(n + "rinv%d" % i, [128, 4], F32) for i in range(2)]; r_rinv = [Res(), Res()]
                og = sb2(n + "og", [128, D], F32); r_og = Res()
                ogT = sb2(n + "ogT", [128, 8, 128], BF16); r_ogT = Res()
                xres = sb2(n + "xres", [128, D], F32); r_xres = Res()
                rr = sb2(n + "rr", [128, D], F32); r_rr = Res()
                xn = sb2(n + "xn", [128, D], F32); r_xn = Res()
                xnew = sb2(n + "xnew", [128, D], F32); r_xnew = Res()
                xTst = sb2(n + "xTst", [128, 8, 128], BF16); r_xTst = Res()
                sm = sb2(n + "sm", [128, 16], F32); r_sm = Res()
                sps = [ps2(n + "sps%d" % i, [128, 1024], F32) for i in range(2)]; r_sps = [Res(), Res()]
                ops_ = [ps2(n + "ops%d" % i, [128, 512], F32) for i in range(2)]; r_ops = [Res(), Res()]
                pp = [ps2(n + "pp%d" % i, [128, 512], F32) for i in range(2)]; r_pp = [Res(), Res()]

                p.dma("pool", wi[:], E.b_win[l].ap().rearrange("(k p) n -> p k n", p=128), (), [r_wi], n + "wi")
                p.dma("pool", wo[:], E.b_wout[l].ap().rearrange("(k p) n -> p k n", p=128), (), [r_wo], n + "w")
                p.dma("sp", lng[:], E.b_lng[l].ap(), (), [r_ln], n + "ln")
                p.dma("sp", lnb[:], E.b_lnb[l].ap(), (), [r_ln], n + "ln")
                p.dma("sp", bias[:].rearrange("p a b c -> p (a b c)"), E.b_bias[l].ap(), (), [r_bias], n + "bias")
                p.dma("sp", bcon[:], E.b_bconst[l].ap(), (), [r_bias], n + "bias")
                bflat = bias[:].rearrange("p a b c -> p (a b c)")
                p.act(bflat, bflat, AF.Exp, [r_bias], [r_bias])

                pin, r_pin = E.park[(1 + l) % 2], E.r_park[(1 + l) % 2]
                if l == 0 and pin0 is not None:
                    pin, r_pin = pin0, Res()
                pout, r_pout = E.park[l % 2], E.r_park[l % 2]
                cnt = [0]

                def nextpp():
                    k = cnt[0] % 2
                    cnt[0] += 1
                    return pp[k], r_pp[k]
                tpv = sps[1][:, :].rearrange("p (k t) -> p k t", t=128)

                def q_proj(g):
                    xsrc_ap = E.xg_in[g].ap() if l == 0 else E.xTb.ap()[:, g * 512:(g + 1) * 512]
                    p.dma("sp", xt[:], xsrc_ap.rearrange("(k p) t -> p k t", p=128), [E.r_xg_in[g] if l == 0 else E.r_xTb], [r_xt], n + "xt")
                    for fb in range(8):
                        pb, rpb = nextpp()
                        for k in range(8):
                            p.mm(pb[:], wi[:, k, fb * 128:(fb + 1) * 128], xt[:, k, :], k == 0, k == 7, [r_wi, r_xt], [rpb])
                        p.act(QT[:, fb, :], pb[:], AF.Identity, [rpb], [r_QT], scale=0.125)

                def emit_scores(qb, h):
                    ts_ = slice((qb % 4) * 128, (qb % 4 + 1) * 128)
                    fb, po = h // 2, (h % 2) * 64
                    sb_i = h % 2
                    sp_, rsp = sps[sb_i], r_sps[sb_i]
                    P_, rP = PT[sb_i], r_PT[sb_i]
                    for j in range(5):
                        kb = qb + j
                        p.mm(sp_[:, j * 128:(j + 1) * 128], KT[po:po + 64, fb, kb * 128:(kb + 1) * 128],
                             QT[po:po + 64, fb, ts_], True, True, [r_KT[kb // 4], r_QT], [rsp])
                    p.act(P_[:, 0:3, :], sp_[:, 0:384].rearrange("p (j t) -> p j t", t=128), AF.Exp, [rsp, r_bias], [rP],
                          bias=bcon[:, h:h + 1])
                    p.memset("dve", P_[0:64, 0, 64:128], 0.0, [rP])
                    p.act(stmp[sb_i][:, 0:128], sp_[:, 384:512], AF.Exp, [rsp], [r_stmp[sb_i]])
                    p.act(stmp[sb_i][:, 128:256], sp_[:, 512:640], AF.Exp, [rsp], [r_stmp[sb_i]])
                    p.tt("dve", P_[:, 3:5, :], stmp[sb_i][:].rearrange("p (j t) -> p j t", t=128), bias[:, h, :, :], ALU.mult,
                         [r_stmp[sb_i], r_bias], [rP])

                def emit_pv(qb, h):
                    hg4, hh = h // 4, h % 4
                    ob, rob = ops_[hg4 % 2], r_ops[hg4 % 2]
                    P_, rP = PT[h % 2], r_PT[h % 2]
                    for j in range(5):
                        kb = qb + j
                        p.mm(ob[:, hh * 66:(hh + 1) * 66], P_[:, j, :], V[:, kb, h, :], j == 0, j == 4, [rP, r_V[kb]], [rob])
                    if hh == 3:
                        obv = ob[:, 0:264].rearrange("p (h d) -> p h d", d=66)
                        ri, rri = rinv[hg4 % 2], r_rinv[hg4 % 2]
                        p.op("dve", lambda e, ri=ri, obv=obv: e.reciprocal(out=ri[:].rearrange("p (h o) -> p h o", o=1),
                                                                          in_=obv[:, :, 64:65]), [rob], [rri])
                        for h2 in range(4):
                            hx = hg4 * 4 + h2
                            p.stt("dve", og[:, hx * 64:(hx + 1) * 64], obv[:, h2, 0:64], ri[:, h2:h2 + 1], sg[:, hx * 64:(hx + 1) * 64],
                                  ALU.mult, ALU.mult, [rob, rri, r_sg], [r_og])

                def heads(qb, lo, hi):
                    for idx in range(lo, hi):
                        if idx < 16:
                            emit_scores(qb, idx)
                        if idx >= 1:
                            emit_pv(qb, idx - 1)

                def a_start(qb):
                    ts_ = slice((qb % 4) * 128, (qb % 4 + 1) * 128)
                    p.dma("sp", xres[:], pin.ap()[qb * 128:(qb + 1) * 128, :], [r_pin], [r_xres], n + "xr")
                    for half in range(2):
                        pb, rpb = nextpp()
                        for k in range(8):
                            p.mm(pb[:], xt[:, k, ts_], wi[:, k, 1024 + half * 512:1024 + (half + 1) * 512], k == 0, k == 7,
                                 [r_wi, r_xt], [rpb])
                        p.act(sg[:, half * 512:(half + 1) * 512], pb[:], AF.Silu, [rpb], [r_sg])
                    heads(qb, 0, NSPLIT)

                def tail1(qb):
                    for k in range(8):
                        p.tr(tpv[:, k, :], og[:, k * 128:(k + 1) * 128], E.identf[:], [r_og, E.r_const], [r_sps[1]])
                    p.cp("act", ogT[:], tpv, [r_sps[1]], [r_ogT])
                    yb = sps[0]; ryb = r_sps[0]
                    for half in range(2):
                        for k in range(8):
                            p.mm(yb[:, half * 512:(half + 1) * 512], ogT[:, k, :], wo[:, k, half * 512:(half + 1) * 512], k == 0, k == 7,
                                 [r_ogT, r_wo], [ryb])
                    p.stt("dve", rr[:], xres[:], ALPHA, yb[:], ALU.mult, ALU.add, [r_xres, ryb], [r_rr])
                    ln_tail(E, p, rr, r_rr, lng, lnb, r_ln, xn, r_xn, xnew, r_xnew, sm, r_sm,
                            xnew, r_xnew, E.identf[:], tpv, r_sps[1], xTst, r_xTst, part="a")

                def tail2(qb):
                    ln_tail(E, p, rr, r_rr, lng, lnb, r_ln, xn, r_xn, xnew, r_xnew, sm, r_sm,
                            xnew, r_xnew, E.identf[:], tpv, r_sps[1], xTst, r_xTst, part="b")
                    if final:
                        toks.append(p.dma("sp", E.out.ap()[qb * 128:(qb + 1) * 128, :], xnew[:], [r_xnew], (), n + "out"))
                    else:
                        p.dma("sp", pout.ap()[qb * 128:(qb + 1) * 128, :], xnew[:], [r_xnew], [r_pout], n + "pk")
                        p.dma("sp", E.xTb.ap()[:, qb * 128:(qb + 1) * 128].rearrange("(k p) t -> p k t", p=128), xTst[:],
                              [r_xTst], [E.r_xTb], n + "xg")

                for qb in range(16):
                    if qb % 4 == 0:
                        q_proj(qb // 4)
                    a_start(qb)
                    if qb > 0:
                        tail2(qb - 1)
                    heads(qb, NSPLIT, 17)
                    tail1(qb)
                tail2(15)
                p.barrier()
    return toks


def _rep(vv):
    return np.ascontiguousarray(np.broadcast_to(np.asarray(vv, np.float32)[None, :], (128, vv.shape[0])))


def prep_inputs(inp, c):
    b, r = c // 4, c % 4
    f = lambda a: np.ascontiguousarray(np.asarray(a, dtype=np.float32))
    m = {}
    x = np.asarray(inp["x"], np.float32)
    m["xq0"] = f(x[b, r * 2048:(r + 1) * 2048])
    ci = np.zeros((1, 8), np.int32)
    ci[0, 0] = r * 512
    ci[0, 4] = max(r - 1, 0) * D * 512
    m["cinfo"] = ci
    s = np.arange(128)
    m["c_mask"] = f((s[:, None] <= s[None, :]))
    m["c_ident"] = f(np.eye(128))
    h = r
    for l in range(2):
        w = np.asarray(inp["a_w_in"][l], np.float32)
        m["a_win%d" % l] = f(np.concatenate([w[:, h * 512:(h + 1) * 512], w[:, 2048 + h * 512:2048 + (h + 1) * 512],
                                             w[:, 4096 + h * 512:4096 + (h + 1) * 512], w[:, 6144 + h:6145 + h],
                                             w[:, 6148 + h:6149 + h]], axis=1))
        m["a_wq%d" % l] = f(inp["a_w_q"][l][h])
        m["a_wk%d" % l] = f(inp["a_w_k"][l][h])
        m["a_wv%d" % l] = f(inp["a_w_v"][l][h])
        vec = np.zeros((128, NVEC), np.float32)
        cw = np.asarray(inp["a_conv_w"][l], np.float32)[:, h * 512:(h + 1) * 512]
        for blk in range(4):
            for tap in range(4):
                vec[:, CW + 4 * blk + tap] = cw[tap, blk * 128:(blk + 1) * 128]
            vec[:, CB + blk] = np.asarray(inp["a_conv_b"][l], np.float32)[h * 512 + blk * 128:h * 512 + (blk + 1) * 128]
            vec[:, GNW + blk] = np.asarray(inp["a_gn_w"][l], np.float32)[h * 512 + blk * 128:h * 512 + (blk + 1) * 128]
            vec[:, SK + blk] = np.asarray(inp["a_skip"][l], np.float32)[h * 512 + blk * 128:h * 512 + (blk + 1) * 128]
        vec[:, BI] = np.asarray(inp["a_b_gate"][l], np.float32)[h]
        vec[:, BF] = np.asarray(inp["a_b_gate"][l], np.float32)[4 + h]
        m["a_vec%d" % l] = vec
        m["a_wout%d" % l] = f(inp["a_w_out"][l])
        m["a_lng%d" % l] = _rep(np.asarray(inp["a_ln_g"][l]))
        m["a_lnb%d" % l] = _rep(np.asarray(inp["a_ln_b"][l]))
    m["kv_w"] = f(inp["kv_w"])
    kk = np.arange(640)[:, None]
    tq = np.arange(128)[None, :]
    ci_ = tq // 64
    valid = (kk >= 64 * ci_) & (kk < 64 * ci_ + 576)
    idx = np.clip(tq + 512 - kk, -128, 128) + 128
    for l in range(2):
        m["b_win%d" % l] = f(inp["b_w_in"][l])
        m["b_wout%d" % l] = f(inp["b_w_out"][l])
        m["b_lng%d" % l] = _rep(np.asarray(inp["b_ln_g"][l]))
        m["b_lnb%d" % l] = _rep(np.asarray(inp["b_ln_b"][l]))
        rb = np.asarray(inp["b_rel_bias"][l], np.float32)
        bt = np.where(valid[None], rb[:, idx], np.float32(NEG)).astype(np.float32)
        bt = bt.reshape(16, 5, 128, 128)[:, 3:5]
        m["b_bias%d" % l] = f(bt.transpose(2, 0, 1, 3).reshape(128, 16 * 2 * 128))
        m["b_bconst%d" % l] = _rep(rb[:, 256])
    kx = np.arange(2560).reshape(20, 128).T
    kvd = ((kx + r * 2048 - 512) >= 0).astype(np.float32)
    m["kvalid"] = f(np.repeat(kvd[:, :, None], 16, axis=2).reshape(128, 320))
    return m


_CACHE = {}


def kernel(**inputs):
    if "nc" not in _CACHE:
        _CACHE["nc"] = build(4)
    nc, E = _CACHE["nc"]
    in_maps = [prep_inputs(inputs, c) for c in range(8)]
    res = run_bass_kernel_spmd(nc, in_maps, core_ids=list(range(8)))
    out = np.zeros((2, S, D), np.float32)
    for c in range(8):
        b, r = c // 4, c % 4
        out[b, r * 2048:(r + 1) * 2048] = np.asarray(res.results[c]["out"], np.float32)
    return out
```
